# Optimizing a Trainium2 kernel written in Bass

```python
import math
import jax, jax.numpy as jnp
from jax import lax
import numpy as np

D_MODEL = 2048
BATCH = 2
SEQ = 4096
DEPTH = 2
DEC_BATCH = 16
DEC_SEQ = 16
PAST_LEN = 2048

CHUNK = 64
HEAD_DIM = 64
SWA_WINDOW = 128
SWA_BACK = SWA_WINDOW // CHUNK
SWA_HEADS = 16
SWA_KV_HEADS = 2
T5_BUCKETS = 32
T5_MAX_DIST = 128
RET_HEADS = 8
RET_DK = 64
RET_DV = 128
ROPE_BASE = 10000.0
RET_NORM_EPS = 1e-5
BAND_BACK = 8
BAND_HEADS = 16
BAND_MAX_REL = 256
MEM_LEN = 256
MEM_HEADS = 4
MEM_HD = D_MODEL // MEM_HEADS
MEM_W = MEM_HEADS * MEM_HD
D_FF = 5632
CONV_W = 3
DN_ALPHA = (2 * DEPTH) ** 0.25
DN_BETA = (8 * DEPTH) ** -0.25
LN_EPS = 1e-5

SWA_Q_W = SWA_HEADS * HEAD_DIM
SWA_KV_W = SWA_KV_HEADS * HEAD_DIM
RET_QK_W = RET_HEADS * RET_DK
RET_V_W = RET_HEADS * RET_DV
BAND_W = BAND_HEADS * HEAD_DIM
IN_WIDTHS = (SWA_Q_W, SWA_KV_W, SWA_KV_W, RET_QK_W, RET_QK_W, RET_V_W, RET_V_W,
             BAND_W, BAND_W, BAND_W, D_MODEL, D_MODEL, D_MODEL)
IN_COLS = sum(IN_WIDTHS)

kernel_name = "hybrid_streaming_encoder_step"

F32 = jnp.float32


def layer_norm(x, g, b):
    xf = x.astype(F32)
    mu = xf.mean(-1, keepdims=True)
    var = jnp.mean(jnp.square(xf - mu), -1, keepdims=True)
    return ((xf - mu) * lax.rsqrt(var + LN_EPS) * g.astype(F32) + b.astype(F32)).astype(x.dtype)


def rope(x, pos):
    half = x.shape[-1] // 2
    inv = ROPE_BASE ** (-jnp.arange(half, dtype=F32) / half)
    ang = pos.astype(F32)[:, None] * inv[None, :]
    cos = jnp.cos(ang)[None, :, None, :]
    sin = jnp.sin(ang)[None, :, None, :]
    x1 = x[..., :half].astype(F32)
    x2 = x[..., half:].astype(F32)
    return jnp.concatenate([x1 * cos - x2 * sin, x2 * cos + x1 * sin], -1).astype(x.dtype)


def t5_bucket(rel):
    nb = T5_BUCKETS // 2
    max_exact = nb // 2
    n = jnp.abs(rel)
    nf = jnp.maximum(n, 1).astype(F32)
    large = max_exact + (jnp.log(nf / max_exact) / math.log(T5_MAX_DIST / max_exact)
                         * (nb - max_exact)).astype(jnp.int32)
    large = jnp.minimum(large, nb - 1)
    return jnp.where(rel > 0, nb, 0) + jnp.where(n < max_exact, n, large)


def t5_bias(table, rel):
    return jnp.transpose(table[t5_bucket(rel)], (2, 0, 1)).astype(F32)


def clipped_rel_bias(table, rel):
    return table[:, jnp.clip(rel, -BAND_MAX_REL, BAND_MAX_REL) + BAND_MAX_REL].astype(F32)


def sink_softmax(s, sink):
    m = s.max(-1, keepdims=True)
    if sink is not None:
        m = jnp.maximum(m, sink)
    p = jnp.exp(s - m)
    den = p.sum(-1, keepdims=True)
    if sink is not None:
        den = den + jnp.exp(sink - m)
    return p / den


def band_attention(q, k, v, n_back, bias, sink):
    B, S, H, dh = q.shape
    G = k.shape[2]
    R = H // G
    nC = S // CHUNK
    qc = q.reshape(B, nC, CHUNK, G, R, dh)
    pad = ((0, 0), (n_back * CHUNK, 0), (0, 0), (0, 0))
    kc = jnp.pad(k, pad).reshape(B, nC + n_back, CHUNK, G, dh)
    vc = jnp.pad(v, pad).reshape(B, nC + n_back, CHUNK, G, dh)
    s = jnp.concatenate([jnp.einsum('bnqgrd,bnkgd->bngrqk', qc, kc[:, j:j + nC])
                         for j in range(n_back + 1)], axis=-1).astype(F32) * (dh ** -0.5)
    s = s + bias.reshape(G, R, CHUNK, -1)
    chunk_ok = (jnp.arange(nC)[:, None] + jnp.arange(n_back + 1)[None, :]) >= n_back
    key_ok = jnp.repeat(chunk_ok, CHUNK, axis=1)
    s = jnp.where(key_ok[None, :, None, None, None, :], s, -jnp.inf)
    sk = None if sink is None else sink.astype(F32).reshape(G, R, 1, 1)
    p = sink_softmax(s, sk).astype(v.dtype)
    o = sum(jnp.einsum('bngrqk,bnkgd->bnqgrd', p[..., j * CHUNK:(j + 1) * CHUNK], vc[:, j:j + nC])
            for j in range(n_back + 1))
    return o.reshape(B, S, H * dh)


def cached_attention(q, k, v, bias, sink):
    B, n, H, dh = q.shape
    G = k.shape[2]
    R = H // G
    s = jnp.einsum('bqgrd,bkgd->bgrqk', q.reshape(B, n, G, R, dh), k).astype(F32) * (dh ** -0.5)
    s = s + bias.reshape(G, R, n, -1)
    sk = None if sink is None else sink.astype(F32).reshape(G, R, 1, 1)
    p = sink_softmax(s, sk).astype(v.dtype)
    return jnp.einsum('bgrqk,bkgd->bqgrd', p, v).reshape(B, n, H * dh)


def ret_log_decay():
    return jnp.log(1.0 - 2.0 ** (-5.0 - jnp.arange(RET_HEADS, dtype=F32)))


def ret_block_output(q, k, v, s0, log_g):
    L = q.shape[2]
    i = jnp.arange(L, dtype=F32)
    diff = i[:, None] - i[None, :]
    decay = jnp.where(diff >= 0, jnp.exp(log_g[:, None, None] * jnp.maximum(diff, 0.0)), 0.0)
    qk = jnp.einsum('bnihd,bnjhd->bnhij', q, k).astype(F32) * decay
    o = jnp.einsum('bnhij,bnjhe->bnihe', qk, v.astype(F32))
    cross = jnp.exp(log_g[None, :] * (i[:, None] + 1.0))
    return o + jnp.einsum('bnihd,bnhde->bnihe', q.astype(F32), s0) * cross[:, :, None]


def ret_block_update(k, v, log_g):
    L = k.shape[2]
    i = jnp.arange(L, dtype=F32)
    w = jnp.exp(log_g[None, :] * (L - 1.0 - i)[:, None])
    return jnp.einsum('bnjhd,bnjhe->bnhde', k.astype(F32) * w[:, :, None], v.astype(F32))


def ret_finish(o, gate):
    mu = o.mean(-1, keepdims=True)
    var = jnp.mean(jnp.square(o - mu), -1, keepdims=True)
    o = (o - mu) * lax.rsqrt(var + RET_NORM_EPS)
    B, T = o.shape[:2]
    return (o.reshape(B, T, -1) * jax.nn.silu(gate.astype(F32))).astype(gate.dtype)


def project_in(x, w_in):
    B, T, _ = x.shape
    points = [int(p) for p in np.cumsum(IN_WIDTHS)[:-1]]
    qa, ka, va, qb, kb, vb, gr, qc, kc, vc, ga, gb, gc = jnp.split(x @ w_in, points, axis=-1)
    hd = lambda t, h: t.reshape(B, T, h, -1)
    return (hd(qa, SWA_HEADS), hd(ka, SWA_KV_HEADS), hd(va, SWA_KV_HEADS),
            hd(qb, RET_HEADS), hd(kb, RET_HEADS), hd(vb, RET_HEADS), gr,
            hd(qc, BAND_HEADS), hd(kc, BAND_HEADS), hd(vc, BAND_HEADS), ga, gb, gc)


def merge_branches(oa, ob, oc, ga, gb, gc, w_br_a, w_br_b, w_br_c, w_mix_o):
    mix = (jax.nn.sigmoid(ga) * (oa @ w_br_a) + jax.nn.sigmoid(gb) * (ob @ w_br_b)
           + jax.nn.sigmoid(gc) * (oc @ w_br_c))
    return mix @ w_mix_o


def mixer_prompt(x, w_in, t5_table, sink, band_table, w_br_a, w_br_b, w_br_c, w_mix_o):
    B, S, _ = x.shape
    qa, ka, va, qb, kb, vb, gr, qc, kc, vc, ga, gb, gc = project_in(x, w_in)
    nka = (SWA_BACK + 1) * CHUNK
    rel_a = jnp.arange(nka)[None, :] - SWA_BACK * CHUNK - jnp.arange(CHUNK)[:, None]
    oa = band_attention(qa, ka, va, SWA_BACK, t5_bias(t5_table, rel_a), sink)
    pos = jnp.arange(S)
    qb = rope(qb, pos)
    kb = rope(kb, pos) * (RET_DK ** -0.5)
    nC = S // CHUNK
    ch = lambda t: t.reshape(B, nC, CHUNK, RET_HEADS, -1)
    qr, kr, vr = ch(qb), ch(kb), ch(vb)
    log_g = ret_log_decay()
    upd = ret_block_update(kr, vr, log_g)
    g_chunk = jnp.exp(log_g * CHUNK)[:, None, None]

    def step(s, u):
        return g_chunk * s + u, s

    s_final, s_start = lax.scan(step, jnp.zeros((B, RET_HEADS, RET_DK, RET_DV), F32),
                                jnp.moveaxis(upd, 1, 0))
    o_r = ret_block_output(qr, kr, vr, jnp.moveaxis(s_start, 0, 1), log_g)
    ob = ret_finish(o_r.reshape(B, S, RET_HEADS, RET_DV), gr)
    nkc = (BAND_BACK + 1) * CHUNK
    rel_c = jnp.arange(nkc)[None, :] - BAND_BACK * CHUNK - jnp.arange(CHUNK)[:, None]
    oc = band_attention(qc, kc, vc, BAND_BACK, clipped_rel_bias(band_table, rel_c), None)
    out = merge_branches(oa, ob, oc, ga, gb, gc, w_br_a, w_br_b, w_br_c, w_mix_o)
    la = min(SWA_WINDOW, S)
    lc = min(BAND_BACK * CHUNK, S)
    return out, (ka[:, S - la:], va[:, S - la:], s_final, kc[:, S - lc:], vc[:, S - lc:])


def mixer_sample(x, swa_k, swa_v, ret_s, band_k, band_v, w_in, t5_table, sink, band_table,
                 w_br_a, w_br_b, w_br_c, w_mix_o):
    B, n, _ = x.shape
    qa, ka, va, qb, kb, vb, gr, qc, kc, vc, ga, gb, gc = project_in(x, w_in)
    qpos = PAST_LEN + jnp.arange(n)
    la = swa_k.shape[1]
    kpos_a = jnp.concatenate([PAST_LEN - la + jnp.arange(la), qpos])
    ka_all = jnp.concatenate([swa_k.astype(ka.dtype), ka], 1)
    va_all = jnp.concatenate([swa_v.astype(va.dtype), va], 1)
    oa = cached_attention(qa, ka_all, va_all, t5_bias(t5_table, kpos_a[None, :] - qpos[:, None]), sink)
    qb = rope(qb, qpos)
    kb = rope(kb, qpos) * (RET_DK ** -0.5)
    log_g = ret_log_decay()
    s0 = ret_s.astype(F32)
    o_r = ret_block_output(qb[:, None], kb[:, None], vb[:, None], s0[:, None], log_g)[:, 0]
    s_new = jnp.exp(log_g * n)[:, None, None] * s0 + ret_block_update(kb[:, None], vb[:, None], log_g)[:, 0]
    ob = ret_finish(o_r, gr)
    lc = band_k.shape[1]
    kpos_c = jnp.concatenate([PAST_LEN - lc + jnp.arange(lc), qpos])
    kc_all = jnp.concatenate([band_k.astype(kc.dtype), kc], 1)
    vc_all = jnp.concatenate([band_v.astype(vc.dtype), vc], 1)
    oc = cached_attention(qc, kc_all, vc_all,
                          clipped_rel_bias(band_table, kpos_c[None, :] - qpos[:, None]), None)
    out = merge_branches(oa, ob, oc, ga, gb, gc, w_br_a, w_br_b, w_br_c, w_mix_o)
    return out, (ka, va, s_new, kc, vc)


def mem_kv(mem, w_mk, w_mv):
    B, M, _ = mem.shape
    return ((mem @ w_mk).reshape(B, M, MEM_HEADS, MEM_HD), (mem @ w_mv).reshape(B, M, MEM_HEADS, MEM_HD))


def mem_attend(x, mk, mv, w_mq, w_mo):
    B, T, _ = x.shape
    q = (x @ w_mq).reshape(B, T, MEM_HEADS, MEM_HD)
    s = jnp.einsum('bqhd,bkhd->bhqk', q, mk.astype(q.dtype)).astype(F32) * (MEM_HD ** -0.5)
    p = jax.nn.softmax(s, axis=-1).astype(q.dtype)
    o = jnp.einsum('bhqk,bkhd->bqhd', p, mv.astype(q.dtype)).reshape(B, T, MEM_W)
    return o @ w_mo


def conv_ffn(x, conv_state, w_ffn_in, conv_w, conv_b, w_ffn_out):
    u = x @ w_ffn_in
    T = u.shape[1]
    up = jnp.concatenate([conv_state.astype(u.dtype), u], 1)
    c = sum(conv_w[j] * up[:, j:j + T] for j in range(CONV_W)) + conv_b
    g, val = jnp.split(c, 2, axis=-1)
    h = jax.nn.gelu(g, approximate=False) * val
    return h @ w_ffn_out, up[:, T:]


def setup_inputs(seed: int = 0) -> dict:
    key = jax.random.key(seed)
    ks = iter(jax.random.split(key, 40))
    nrm = lambda shape, scale: jax.random.normal(next(ks), shape, F32) * scale
    swa_cache = min(SWA_WINDOW, PAST_LEN)
    band_cache = min(BAND_BACK * CHUNK, PAST_LEN)
    F2 = 2 * D_FF
    D = D_MODEL
    return {
        "x_prompt": nrm((BATCH, SEQ, D), 1.0),
        "x_sample": nrm((DEC_BATCH, DEC_SEQ, D), 1.0),
        "mem_prompt": nrm((BATCH, MEM_LEN, D), 1.0),
        "cache_swa_k": nrm((DEPTH, DEC_BATCH, swa_cache, SWA_KV_HEADS, HEAD_DIM), 1.0),
        "cache_swa_v": nrm((DEPTH, DEC_BATCH, swa_cache, SWA_KV_HEADS, HEAD_DIM), 1.0),
        "state_ret": nrm((DEPTH, DEC_BATCH, RET_HEADS, RET_DK, RET_DV), 0.5),
        "cache_band_k": nrm((DEPTH, DEC_BATCH, band_cache, BAND_HEADS, HEAD_DIM), 1.0),
        "cache_band_v": nrm((DEPTH, DEC_BATCH, band_cache, BAND_HEADS, HEAD_DIM), 1.0),
        "state_ffn_conv": nrm((DEPTH, DEC_BATCH, CONV_W - 1, F2), 1.0),
        "cache_mem_k": nrm((DEPTH, DEC_BATCH, MEM_LEN, MEM_HEADS, MEM_HD), 1.0),
        "cache_mem_v": nrm((DEPTH, DEC_BATCH, MEM_LEN, MEM_HEADS, MEM_HD), 1.0),
        "w_in": nrm((DEPTH, D, IN_COLS), D ** -0.5),
        "t5_table": nrm((T5_BUCKETS, SWA_HEADS), 0.2),
        "swa_sink": nrm((DEPTH, SWA_HEADS), 0.5),
        "band_rel_table": nrm((DEPTH, BAND_HEADS, 2 * BAND_MAX_REL + 1), 0.2),
        "w_br_a": nrm((DEPTH, SWA_Q_W, D), SWA_Q_W ** -0.5),
        "w_br_b": nrm((DEPTH, RET_V_W, D), RET_V_W ** -0.5),
        "w_br_c": nrm((DEPTH, BAND_W, D), BAND_W ** -0.5),
        "w_mix_o": nrm((DEPTH, D, D), D ** -0.5 * DN_BETA),
        "ln1_g": 1.0 + nrm((DEPTH, D), 0.01),
        "ln1_b": nrm((DEPTH, D), 0.01),
        "w_mq": nrm((DEPTH, D, MEM_W), D ** -0.5),
        "w_mk": nrm((DEPTH, D, MEM_W), D ** -0.5),
        "w_mv": nrm((DEPTH, D, MEM_W), D ** -0.5),
        "w_mo": nrm((DEPTH, MEM_W, D), MEM_W ** -0.5 * DN_BETA),
        "ln2_g": 1.0 + nrm((DEPTH, D), 0.01),
        "ln2_b": nrm((DEPTH, D), 0.01),
        "w_ffn_in": nrm((DEPTH, D, F2), D ** -0.5),
        "ffn_conv_w": nrm((DEPTH, CONV_W, F2), CONV_W ** -0.5),
        "ffn_conv_b": nrm((DEPTH, F2), 0.01),
        "w_ffn_out": nrm((DEPTH, D_FF, D), D_FF ** -0.5 * DN_BETA),
        "ln3_g": 1.0 + nrm((DEPTH, D), 0.01),
        "ln3_b": nrm((DEPTH, D), 0.01),
    }


def reference(x_prompt, x_sample, mem_prompt, cache_swa_k, cache_swa_v, state_ret, cache_band_k,
              cache_band_v, state_ffn_conv, cache_mem_k, cache_mem_v, w_in, t5_table, swa_sink,
              band_rel_table, w_br_a, w_br_b, w_br_c, w_mix_o, ln1_g, ln1_b, w_mq, w_mk, w_mv, w_mo,
              ln2_g, ln2_b, w_ffn_in, ffn_conv_w, ffn_conv_b, w_ffn_out, ln3_g, ln3_b):
    xp = x_prompt
    xs = x_sample
    Bp = xp.shape[0]
    p_ak, p_av, p_rs, p_bk, p_bv, p_fc, p_mk, p_mv = [], [], [], [], [], [], [], []
    s_ak, s_av, s_rs, s_bk, s_bv, s_fc = [], [], [], [], [], []
    for l in range(DEPTH):
        mix, (ak, av, rs, bk, bv) = mixer_prompt(xp, w_in[l], t5_table, swa_sink[l], band_rel_table[l],
                                                 w_br_a[l], w_br_b[l], w_br_c[l], w_mix_o[l])
        xp = layer_norm(DN_ALPHA * xp + mix, ln1_g[l], ln1_b[l])
        mk, mv = mem_kv(mem_prompt, w_mk[l], w_mv[l])
        xp = layer_norm(DN_ALPHA * xp + mem_attend(xp, mk, mv, w_mq[l], w_mo[l]), ln2_g[l], ln2_b[l])
        f, fc = conv_ffn(xp, jnp.zeros((Bp, CONV_W - 1, 2 * D_FF), xp.dtype), w_ffn_in[l],
                         ffn_conv_w[l], ffn_conv_b[l], w_ffn_out[l])
        xp = layer_norm(DN_ALPHA * xp + f, ln3_g[l], ln3_b[l])
        p_ak.append(ak); p_av.append(av); p_rs.append(rs); p_bk.append(bk); p_bv.append(bv)
        p_fc.append(fc); p_mk.append(mk); p_mv.append(mv)
        mix, (ak, av, rs, bk, bv) = mixer_sample(xs, cache_swa_k[l], cache_swa_v[l], state_ret[l],
                                                 cache_band_k[l], cache_band_v[l], w_in[l], t5_table,
                                                 swa_sink[l], band_rel_table[l], w_br_a[l], w_br_b[l],
                                                 w_br_c[l], w_mix_o[l])
        xs = layer_norm(DN_ALPHA * xs + mix, ln1_g[l], ln1_b[l])
        xs = layer_norm(DN_ALPHA * xs + mem_attend(xs, cache_mem_k[l], cache_mem_v[l], w_mq[l], w_mo[l]),
                        ln2_g[l], ln2_b[l])
        f, fc = conv_ffn(xs, state_ffn_conv[l], w_ffn_in[l], ffn_conv_w[l], ffn_conv_b[l], w_ffn_out[l])
        xs = layer_norm(DN_ALPHA * xs + f, ln3_g[l], ln3_b[l])
        s_ak.append(ak); s_av.append(av); s_rs.append(rs); s_bk.append(bk); s_bv.append(bv)
        s_fc.append(fc)
    st = lambda a: jnp.stack(a, 0)
    return (xp, xs,
            st(p_ak), st(p_av), st(p_rs), st(p_bk), st(p_bv), st(p_fc), st(p_mk), st(p_mv),
            st(s_ak), st(s_av), st(s_rs), st(s_bk), st(s_bv), st(s_fc))
```

```python
import os
import math
from contextlib import ExitStack
import numpy as np
import concourse.bass as bass
import concourse.mybir as mybir
from concourse.bass_utils import run_bass_kernel_spmd

F32 = mybir.dt.float32; BF16 = mybir.dt.bfloat16; I32 = mybir.dt.int32
AF = mybir.ActivationFunctionType; ALU = mybir.AluOpType; AX = mybir.AxisListType

D = 2048; KC = 16; TP = 1024; TS = 64; T = TP + TS; HALO = 512; DEPTH = 2
INC = 13568; DFF = 5632; F2 = 2 * DFF
C_QA, C_KA, C_VA, C_QB, C_KB, C_VB, C_GR, C_QC, C_KC, C_VC, C_GA, C_GB, C_GC = (
    0, 1024, 1152, 1280, 1792, 2304, 3328, 4352, 5376, 6400, 7424, 9472, 11520)
ALPHA = (2 * DEPTH) ** 0.25
LOGG = [math.log(1.0 - 2.0 ** (-5.0 - h)) for h in range(8)]
STAGE = int(os.environ.get("MK_STAGE", "99"))


class Res:
    __slots__ = ("w", "r")

    def __init__(s):
        s.w = None; s.r = {}


class Eng:
    def __init__(s, nc, name, h):
        s.name = name; s.h = h; s.sem = nc.alloc_semaphore(name="p_" + name); s.count = 0; s.seen = {}

    def wait(s, tok):
        if tok is None:
            return
        sem, val = tok
        if sem is s.sem and (val > s.count or s.name == "pe"):
            return
        k = id(sem)
        if s.seen.get(k, 0) >= val:
            return
        s.seen[k] = val
        s.h.wait_ge(sem, val)


class FW:
    def __init__(s, nc, ndma=40):
        s.nc = nc
        s.pe = Eng(nc, "pe", nc.tensor); s.act = Eng(nc, "act", nc.scalar); s.dve = Eng(nc, "dve", nc.vector)
        s.pool = Eng(nc, "pool", nc.gpsimd); s.sp = Eng(nc, "sp", nc.sync)
        s.engs = [s.pe, s.act, s.dve, s.pool, s.sp]
        s.dsem = [[nc.alloc_semaphore(name=f"dma{i}"), 0] for i in range(ndma)]; s.di = {"pool": 0, "sp": 0}
        s.npool = 16
        s.out_toks = []

    def _deps(s, eng, reads, writes):
        for r in reads:
            eng.wait(r.w)
        for w in writes:
            eng.wait(w.w)
            for t in w.r.values():
                eng.wait(t)

    def _post(s, tok, reads, writes):
        for r in reads:
            r.r[id(tok[0])] = tok
        for w in writes:
            w.w = tok; w.r = {}

    def op(s, eng, fn, reads=(), writes=(), inc=True):
        s._deps(eng, reads, writes)
        inst = fn(eng.h)
        if inc:
            eng.count += 1; inst.then_inc(eng.sem, 1); tok = (eng.sem, eng.count)
        else:
            tok = (eng.sem, eng.count + 1)
        s._post(tok, reads, writes)
        return tok

    def _dmatok(s, eng):
        if eng.name == "pool":
            i = s.di["pool"]; s.di["pool"] = (i + 1) % s.npool
        else:
            i = s.npool + s.di["sp"]; s.di["sp"] = (s.di["sp"] + 1) % (len(s.dsem) - s.npool)
        d = s.dsem[i]
        if d[1] > 0:
            eng.wait((d[0], d[1]))
        return d

    def dma(s, eng, out, in_, reads=(), writes=(), is_out=False, **kw):
        s._deps(eng, reads, writes)
        d = s._dmatok(eng)
        inst = eng.h.dma_start(out=out, in_=in_, **kw)
        d[1] += 16; inst.then_inc(d[0], 16); tok = (d[0], d[1])
        s._post(tok, reads, writes)
        if is_out:
            s.out_toks.append(tok)
        return tok

    def special(s, eng, fn, reads=(), writes=(), incval=16):
        s._deps(eng, reads, writes)
        s.nspecial = getattr(s, "nspecial", 0) + 1
        sem = s.nc.alloc_semaphore(name=f"spc{s.nspecial}")
        inst = fn(eng.h)
        inst.then_inc(sem, incval); tok = (sem, incval)
        s.spc_toks = getattr(s, "spc_toks", []) + [tok]
        s._post(tok, reads, writes)
        return tok

    def barrier(s):
        toks = [(e.sem, e.count) for e in s.engs if e.count > 0] + [(d[0], d[1]) for d in s.dsem if d[1] > 0] + list(getattr(s, 'spc_toks', []))
        for e in s.engs:
            for t in toks:
                if t[0] is not e.sem:
                    e.wait(t)

    def finish(s):
        for t in s.out_toks:
            s.sp.wait(t)
        s.barrier()


def t5_bucket_np(rel):
    nb = 16; max_exact = 8
    n = np.abs(rel)
    nf = np.maximum(n, 1).astype(np.float32)
    large = max_exact + (np.log(nf / max_exact) / math.log(128 / max_exact) * (nb - max_exact)).astype(np.int32)
    large = np.minimum(large, nb - 1)
    return np.where(rel > 0, nb, 0) + np.where(n < max_exact, n, large)


def rope_tables(pos):
    half = 32
    inv = (10000.0 ** (-np.arange(half, dtype=np.float32) / half)).astype(np.float32)
    ang = pos.astype(np.float32)[:, None] * inv[None, :]
    return np.cos(ang).astype(np.float32), np.sin(ang).astype(np.float32)


def core_consts(c):
    seg = c % 4
    cs = {}
    cs["ident"] = np.eye(128, dtype=np.float32)
    cs["antij"] = np.eye(128, dtype=np.float32)[::-1].copy()
    rt = np.zeros((128, 9, 64), np.float32)
    cp, sp_ = rope_tables(np.arange(seg * 1024, seg * 1024 + 1024))
    rt[:, :8, :32] = cp.reshape(8, 128, 32).transpose(1, 0, 2); rt[:, :8, 32:] = sp_.reshape(8, 128, 32).transpose(1, 0, 2)
    cq, sq = rope_tables(2048 + np.arange(16))
    for b_ in range(2):
        rt[32 * b_:32 * b_ + 16, 8, :32] = cq; rt[32 * b_:32 * b_ + 16, 8, 32:] = sq
    cs["rope"] = rt
    g = np.exp(np.array(LOGG, np.float64))
    jj = np.arange(64)[:, None]; ii = np.arange(64)[None, :]
    dec = np.zeros((128, 8, 64), np.float32)
    for h in range(8):
        m = np.where(ii >= jj, g[h] ** (-(jj + 1.0)), 0.0)
        dec[:64, h] = m; dec[64:, h] = m
    cs["decT"] = dec.reshape(128, 512)
    decs = np.zeros((64, 8, 16), np.float32)
    for h in range(8):
        m = np.where(ii[:16, :16] >= jj[:16], g[h] ** (-(jj[:16] + 1.0)), 0.0)
        decs[0:16, h] = m; decs[32:48, h] = m
    cs["decTs"] = decs.reshape(64, 128)
    kwx = np.zeros((128, 9, 8, 64), np.float32); crx = np.ones((128, 9, 8, 64), np.float32)
    for h in range(8):
        j = np.arange(TP)
        kwx[:, :8, h, :] = (g[h] ** (63 - j % 64)).reshape(8, 128).T[:, :, None]
        crx[:, :8, h, :] = (g[h] ** (j % 64 + 1.0)).reshape(8, 128).T[:, :, None]
        for b in range(2):
            kwx[32 * b:32 * b + 16, 8, h, :] = (g[h] ** (15 - np.arange(16)))[:, None]
            crx[32 * b:32 * b + 16, 8, h, :] = (g[h] ** (np.arange(16) + 1.0))[:, None]
    cs["kwx"] = kwx.reshape(128, 9, 512); cs["crossx"] = crx.reshape(128, 9, 512)
    gx = np.zeros((128, 4, 128), np.float64)
    for h in range(8):
        gx[(h % 2) * 64:(h % 2) * 64 + 64, h // 2, :] = g[h]
    cs["g64x"] = (gx ** 64).reshape(128, 512).astype(np.float32); cs["g16x"] = (gx ** 16).reshape(128, 512).astype(np.float32)
    cs["flag"] = np.full((128, 1), 1.0 if seg > 0 else 0.0, np.float32)
    idx = np.zeros((128, 3), np.int32); coef = np.zeros((128, 3, 4, 128), np.float32)
    for k in range(3):
        pc = c - 1 - k
        valid = (seg - 1 - k) >= 0
        if not valid:
            pc = c
        idx[:, k] = pc * 128 + np.arange(128)
        if valid:
            for h in range(8):
                coef[(h % 2) * 64:(h % 2) * 64 + 64, k, h // 2, :] = g[h] ** (1024.0 * k)
    cs["pidx"] = idx; cs["rcoef"] = coef.reshape(128, 3, 512)
    rel = np.arange(384) - 256
    sel = np.zeros((32, 384), np.float32); sel[t5_bucket_np(rel), np.arange(384)] = 1.0
    cs["t5sel"] = sel
    return cs


def build_program():
    nc = bass.Bass("TRN2", target_bir_lowering=False)
    fw = FW(nc)
    PE, ACT, DVE, POOL, SP = fw.pe, fw.act, fw.dve, fw.pool, fw.sp

    def din(name, shape, dt=F32):
        return nc.dram_tensor(name, list(shape), dt, kind="ExternalInput").ap()

    def dout(name, shape, dt=F32):
        return nc.dram_tensor(name, list(shape), dt, kind="ExternalOutput").ap()

    def dscr(name, shape, dt=F32):
        return nc.dram_tensor(name, list(shape), dt, kind="Internal")

    INSH = {
        "xp": ([HALO + TP, D], F32), "xs": ([TS, D], F32), "mem": ([256, D], F32),
        "c_swa_k": ([2, 2, 128, 128], F32), "c_swa_v": ([2, 2, 128, 128], F32), "c_ret": ([2, 2, 8, 64, 128], F32),
        "c_band_k": ([2, 2, 512, 1024], F32), "c_band_v": ([2, 2, 512, 1024], F32), "c_conv": ([2, 2, 2, F2], F32),
        "c_mem_k": ([2, 2, 256, 2048], F32), "c_mem_v": ([2, 2, 256, 2048], F32),
        "w_in": ([2, D, INC], F32), "t5_table": ([32, 16], F32), "swa_sink": ([2, 16], F32), "band_rel_table": ([2, 16, 513], F32),
        "w_br_a": ([2, 1024, D], F32), "w_br_b": ([2, 1024, D], F32), "w_br_c": ([2, 1024, D], F32), "w_mix_o": ([2, D, D], F32),
        "ln1_g": ([2, D], F32), "ln2_g": ([2, D], F32), "ln3_g": ([2, D], F32), "ln1_b": ([2, D], F32), "ln2_b": ([2, D], F32), "ln3_b": ([2, D], F32),
        "w_mq": ([2, D, D], F32), "w_mk": ([2, D, D], F32), "w_mv": ([2, D, D], F32), "w_mo": ([2, D, D], F32),
        "w_ffn_in": ([2, D, F2], F32), "ffn_conv_w": ([2, 3, F2], F32), "ffn_conv_b": ([2, F2], F32), "w_ffn_out": ([2, DFF, D], F32),
        "ident": ([128, 128], F32), "antij": ([128, 128], F32), "rope": ([128, 9, 64], F32), "decT": ([128, 512], F32), "crossx": ([128, 9, 512], F32), "g64x": ([128, 512], F32), "g16x": ([128, 512], F32),
        "decTs": ([64, 128], F32), "kwx": ([128, 9, 512], F32), "flag": ([128, 1], F32),
        "pidx": ([128, 3], I32), "rcoef": ([128, 3, 512], F32), "t5sel": ([32, 384], F32),
    }

    class _Lazy:
        def __init__(s):
            s.d = {}

        def __getattr__(s, name):
            if name not in s.d:
                sh, dt = INSH[name]
                s.d[name] = din(name, sh, dt)
            return s.d[name]
    I = _Lazy()
    nc._mk_inputs = I.d
    o_yp = dout("o_yp", [TP, D]); o_ys = dout("o_ys", [TS, D])
    o_swa_p = dout("o_swa_p", [2, 2, 128, 128]); o_ret_p = dout("o_ret_p", [2, 8, 64, 128])
    o_band_p = dout("o_band_p", [2, 2, 512, 1024]); o_conv_p = dout("o_conv_p", [2, 2, F2])
    o_mem_p = dout("o_mem_p", [2, 2, 256, 2048])
    o_swa_s = dout("o_swa_s", [2, 2, TS, 128]); o_ret_s = dout("o_ret_s", [2, 2, 8, 64, 128])
    o_band_s = dout("o_band_s", [2, 2, TS, 1024]); o_conv_s = dout("o_conv_s", [2, 2, 2, F2])
    resid = dscr("resid", [T, D])
    fd_t5 = dscr("fd_t5", [16, 384]); fd_band = dscr("fd_band", [16, 768])
    ex_u_b = dscr("ex_u_b", [128, 512]); ex_u_g = dscr("ex_u_g", [1024, 512])
    ex_c_b = dscr("ex_c_b", [128, 32], BF16); ex_c_g = dscr("ex_c_g", [1024, 32], BF16)
    ex_x_b = dscr("ex_x_b", [128, KC * HALO], BF16); ex_x_g = dscr("ex_x_g", [1024, KC * HALO], BF16)

    A = nc.alloc_sbuf_tensor
    es_all = ExitStack()

    _uid = [0]

    def S(es, name, shape, dt):
        _uid[0] += 1
        return es.enter_context(nc.sbuf_tensor(f"{name}_{_uid[0]}", list(shape), dt))

    ident = A("identb", [128, 128], BF16); antij = A("antijf", [128, 128], F32); identf = A("identf", [128, 128], F32)
    xT = A("xT", [128, KC, T], BF16); xT_r = [Res() for _ in range(9)]
    mix_r = Res(); oT_r = Res()
    mixT = None; oT = None
    memT_r = Res()
    NSLOT = 2
    wslot = [A(f"wslot{i}", [128, KC, 512], BF16) for i in range(NSLOT)]; wslot_r = [Res() for _ in range(NSLOT)]
    wsl = [0]
    flag = A("flag_s", [128, 1], F32); pidx = A("pidx_s", [128, 3], I32)
    cres = Res()
    pf = [nc.alloc_psum_tensor(f"pf{i}", [128, 512], F32) for i in range(6)]; pf_r = [Res() for _ in range(6)]
    pb = [nc.alloc_psum_tensor(f"pb{i}", [128, 1024], BF16) for i in range(2)]; pb_r = [Res() for _ in range(2)]
    pfi = [0]; pbi = [0]

    def PF():
        i = pfi[0]; pfi[0] = (i + 1) % 6
        return pf[i], pf_r[i]

    def PB():
        i = pbi[0]; pbi[0] = (i + 1) % 2
        return pb[i], pb_r[i]

    def wload(src_ap, nk, ncols, eng=None):
        i = wsl[0]; wsl[0] = (i + 1) % NSLOT
        fw.dma(POOL, wslot[i][:, 0:nk, 0:ncols], src_ap.rearrange("(k p) n -> p k n", p=128), writes=[wslot_r[i]])
        return wslot[i], wslot_r[i]

    def evac(eng, out, in_, reads, writes):
        if eng is ACT:
            return fw.op(ACT, lambda h: h.copy(out=out, in_=in_), reads=reads, writes=writes)
        return fw.op(eng, lambda h: h.tensor_copy(out=out, in_=in_), reads=reads, writes=writes)

    with nc.allow_non_contiguous_dma(reason="small consts"):
        fw.dma(POOL, ident[:], I.ident, writes=[cres])
        fw.dma(SP, antij[:], I.antij, writes=[cres]); fw.dma(SP, identf[:], I.ident, writes=[cres])
        fw.dma(SP, flag[:], I.flag, writes=[cres]); fw.dma(SP, pidx[:], I.pidx, writes=[cres])
    fw.barrier()

    def transpose_into(dst_fn, src, src_r, nrow, ncol, dst_r, evac_eng=None, evac_fn=None):
        nch = ncol // 128
        for j0 in range(0, nch, 4):
            p, pr = PB()
            n = min(4, nch - j0)
            for j in range(n):
                fw.op(PE, lambda h, j=j: h.transpose(out=p[:, j * 128:j * 128 + nrow], in_=src[0:nrow, (j0 + j) * 128:(j0 + j + 1) * 128],
                                                     identity=ident[0:nrow, 0:nrow]), reads=[src_r, cres], writes=[pr], inc=(j == n - 1))
            for j in range(n):
                e = evac_eng or (DVE if j % 2 == 0 else ACT)
                if evac_fn is not None:
                    evac_fn(j0 + j, p[:, j * 128:j * 128 + nrow], pr)
                else:
                    evac(e, dst_fn(j0 + j), p[:, j * 128:j * 128 + nrow], [pr], [dst_r])

    TOKBLK = [(0, 512), (512, 512), (1024, TS)]

    def proj_ws(W, Wr, nk, m0, msz, rhs_fn, blocks, consume, rhs_res, tilepos=None, pslice=None):
        for (t0, n) in blocks:
            p, pr = PF()
            po = p[0:msz, 0:n] if pslice is None else p[pslice[0]:pslice[0] + msz, 0:n]
            for k in range(nk):
                kw = {} if tilepos is None else {"tile_position": tilepos}
                fw.op(PE, lambda h, k=k: h.matmul(po, lhsT=W[:, k, m0:m0 + msz], rhs=rhs_fn(k, t0, n), start=(k == 0), stop=(k == nk - 1), **kw),
                      reads=[Wr] + rhs_res, writes=[pr], inc=(k == nk - 1))
            consume(p, pr, t0, n)

    def proj_as(W, Wr, nk, c0, ncols, lhs_fn, ntok, consume, lhs_res):
        p, pr = PF()
        for k in range(nk):
            fw.op(PE, lambda h, k=k: h.matmul(p[0:ntok, 0:ncols], lhsT=lhs_fn(k), rhs=W[:, k, c0:c0 + ncols], start=(k == 0), stop=(k == nk - 1)),
                  reads=[Wr] + lhs_res, writes=[pr], inc=(k == nk - 1))
        consume(p, pr)

    TSB = [0, 32]
    NT = [128] * 8 + [TS]
    TOK0 = [i * 128 for i in range(8)] + [TP]

    def bcast(ap_row, n, nparts=128):
        return bass.AP(ap_row.tensor, ap_row.offset, [[0, nparts], [1, n]])

    ln_r = Res()
    ones_b = A("ones_b", [128, 64], BF16); ones_h = A("ones_h", [128, 64], BF16)
    fw.op(DVE, lambda h: h.memset(ones_b[:], 1.0), writes=[cres])
    onef = A("onef", [128, 64], F32)
    fw.op(DVE, lambda h: h.memset(onef[:], 1.0), writes=[cres])
    fw.op(DVE, lambda h: h.tensor_scalar(out=ones_h[:], in0=onef[:], scalar1=flag[:, 0:1], scalar2=None, op0=ALU.mult), reads=[cres], writes=[cres])
    XH = {}; xh_r = Res()
    zscr = dscr("zscr", [T, D]); z_r = [Res() for _ in range(9)]; resid_r = [Res() for _ in range(9)]

    def resid_src(l, which, tt):
        if l == 0 and which == 0:
            return (I.xp[HALO + tt * 128:HALO + (tt + 1) * 128, :] if tt < 8 else I.xs), []
        return resid.ap()[TOK0[tt]:TOK0[tt] + NT[tt], :], [resid_r[tt]]

    def x_to_xT(l):
        with ExitStack() as es:
            xt = [S(es, f"xtm{i}", [128, D], BF16) for i in range(2)]; xr = [Res(), Res()]
            for i in range(13):
                b = i % 2
                if i < 4:
                    src = I.xp[i * 128:(i + 1) * 128, :]; nt = 128
                    dst = lambda j, i=i: XH['t'][:, j, i * 128:(i + 1) * 128]; dr = xh_r
                elif i < 12:
                    tt = i - 4
                    src = I.xp[HALO + tt * 128:HALO + (tt + 1) * 128, :]; nt = 128
                    dst = lambda j, tt=tt: xT[:, j, tt * 128:(tt + 1) * 128]; dr = xT_r[tt]
                else:
                    src = I.xs; nt = TS
                    dst = lambda j: xT[:, j, TP:TP + TS]; dr = xT_r[8]
                fw.dma(POOL, xt[b][0:nt, :], src, writes=[xr[b]])
                transpose_into(dst, xt[b], xr[b], nt, D, dr)
            fw.barrier()

    def dense_to_z(l, which, Wd, nk, actT_fn, act_res):
        with ExitStack() as es:
            rb = [S(es, f"rb{i}", [128, 512], F32) for i in range(3)]; rr = [Res() for _ in range(3)]
            zb = [S(es, f"zb{i}", [128, 512], F32) for i in range(3)]; zr = [Res() for _ in range(3)]
            ctr = 0
            for k0 in range(0, nk, KC):
                kk = min(KC, nk - k0); first = (k0 == 0)
                for cb in range(4):
                    W, Wr = wload(Wd[k0 * 128:(k0 + kk) * 128, cb * 512:(cb + 1) * 512], kk, 512)
                    for tt in range(9):
                        nt = NT[tt]; i = ctr % 3; ctr += 1
                        if first:
                            src, sr = resid_src(l, which, tt)
                        else:
                            src, sr = zscr.ap()[TOK0[tt]:TOK0[tt] + nt, :], [z_r[tt]]
                        fw.dma(SP, rb[i][0:nt, :], src[:, cb * 512:(cb + 1) * 512], reads=sr, writes=[rr[i]])
                        p, pr = PF()
                        for k in range(kk):
                            fw.op(PE, lambda h, k=k: h.matmul(p[0:nt, :], lhsT=actT_fn(k0 + k, TOK0[tt], nt), rhs=W[:, k, 0:512], start=(k == 0), stop=(k == kk - 1)),
                                  reads=[Wr] + act_res, writes=[pr], inc=(k == kk - 1))
                        fw.op(DVE, lambda h: h.scalar_tensor_tensor(out=zb[i][0:nt, :], in0=rb[i][0:nt, :], scalar=float(ALPHA if first else 1.0), in1=p[0:nt, :],
                                                                     op0=ALU.mult, op1=ALU.add), reads=[rr[i], pr], writes=[zr[i]])
                        fw.dma(SP, zscr.ap()[TOK0[tt]:TOK0[tt] + nt, cb * 512:(cb + 1) * 512], zb[i][0:nt, :], reads=[zr[i]], writes=[z_r[tt]])
            fw.barrier()

    def layernorm(l, which, final_out):
        with ExitStack() as es:
            lng = S(es, "lng", [128, D], F32); lnb = S(es, "lnb", [128, D], F32)
            fw.dma(SP, lng[:], bcast(getattr(I, f'ln{which + 1}_g')[l:l + 1, :], D), writes=[ln_r])
            fw.dma(SP, lnb[:], bcast(getattr(I, f'ln{which + 1}_b')[l:l + 1, :], D), writes=[ln_r])
            zt = [S(es, f"zt{i}", [128, D], F32) for i in range(2)]; ztr = [Res(), Res()]
            yt = [S(es, f"yt{i}", [128, D], F32) for i in range(2)]; ytr = [Res(), Res()]
            yb = [S(es, f"yb{i}", [128, D], BF16) for i in range(2)]; ybr = [Res(), Res()]
            st = [S(es, f"st{i}", [128, 8], F32) for i in range(2)]; sr = [Res(), Res()]
            for tt in range(9):
                nt = NT[tt]; i = tt % 2
                fw.dma(SP, zt[i][0:nt, :], zscr.ap()[TOK0[tt]:TOK0[tt] + nt, :], reads=[z_r[tt]], writes=[ztr[i]])
                s = st[i]
                fw.op(DVE, lambda h: h.tensor_reduce(out=s[0:nt, 0:1], in_=zt[i][0:nt, :], axis=AX.X, op=ALU.add), reads=[ztr[i]], writes=[sr[i]])
                fw.op(DVE, lambda h: h.tensor_tensor(out=yt[i][0:nt, :], in0=zt[i][0:nt, :], in1=zt[i][0:nt, :], op=ALU.mult), reads=[ztr[i]], writes=[ytr[i]])
                fw.op(DVE, lambda h: h.tensor_reduce(out=s[0:nt, 1:2], in_=yt[i][0:nt, :], axis=AX.X, op=ALU.add), reads=[ytr[i]], writes=[sr[i]])
                fw.op(DVE, lambda h: h.tensor_scalar(out=s[0:nt, 2:3], in0=s[0:nt, 0:1], scalar1=1.0 / D, scalar2=None, op0=ALU.mult), reads=[sr[i]], writes=[sr[i]])
                fw.op(DVE, lambda h: h.tensor_tensor(out=s[0:nt, 3:4], in0=s[0:nt, 2:3], in1=s[0:nt, 2:3], op=ALU.mult), reads=[sr[i]], writes=[sr[i]])
                fw.op(DVE, lambda h: h.scalar_tensor_tensor(out=s[0:nt, 4:5], in0=s[0:nt, 1:2], scalar=1.0 / D, in1=s[0:nt, 3:4], op0=ALU.mult, op1=ALU.subtract),
                      reads=[sr[i]], writes=[sr[i]])
                fw.op(DVE, lambda h: h.tensor_scalar(out=s[0:nt, 5:6], in0=s[0:nt, 4:5], scalar1=1e-5, scalar2=None, op0=ALU.add), reads=[sr[i]], writes=[sr[i]])
                fw.op(ACT, lambda h: h.activation(out=s[0:nt, 5:6], in_=s[0:nt, 5:6], func=AF.Sqrt), reads=[sr[i]], writes=[sr[i]])
                fw.op(DVE, lambda h: h.reciprocal(out=s[0:nt, 5:6], in_=s[0:nt, 5:6]), reads=[sr[i]], writes=[sr[i]])
                fw.op(DVE, lambda h: h.scalar_tensor_tensor(out=yt[i][0:nt, :], in0=zt[i][0:nt, :], scalar=s[0:nt, 2:3], in1=lng[0:nt, :], op0=ALU.subtract, op1=ALU.mult),
                      reads=[ztr[i], sr[i], ytr[i], ln_r], writes=[ytr[i]])
                fw.op(DVE, lambda h: h.scalar_tensor_tensor(out=yt[i][0:nt, :], in0=yt[i][0:nt, :], scalar=s[0:nt, 5:6], in1=lnb[0:nt, :], op0=ALU.mult, op1=ALU.add),
                      reads=[ytr[i], sr[i], ln_r], writes=[ytr[i]])
                fw.dma(SP, resid.ap()[TOK0[tt]:TOK0[tt] + nt, :], yt[i][0:nt, :], reads=[ytr[i]], writes=[resid_r[tt]])
                if final_out:
                    dst = o_yp[tt * 128:(tt + 1) * 128, :] if tt < 8 else o_ys
                    fw.dma(SP, dst, yt[i][0:nt, :], reads=[ytr[i]], is_out=True)
                fw.op(ACT, lambda h: h.copy(out=yb[i][0:nt, :], in_=yt[i][0:nt, :]), reads=[ytr[i]], writes=[ybr[i]])
                transpose_into(lambda j, tt=tt, nt=nt: xT[:, j, TOK0[tt]:TOK0[tt] + nt], yb[i], ybr[i], nt, D, xT_r[tt])
            fw.barrier()

    def exchange(src_tile, src_res, bounce, gath, ncols, dsts, dst_res):
        br = Res(); gr_ = Res()
        fw.dma(POOL, bounce.ap(), src_tile, reads=[src_res], writes=[br])
        fw.special(POOL, lambda h: h.collective_compute("AllGather", ALU.bypass, replica_groups=[list(range(8))], ins=[bounce.ap()], outs=[gath.ap()]),
                   reads=[br], writes=[gr_], incval=1)
        for k, dst in enumerate(dsts):
            fw.special(POOL, lambda h, k=k, dst=dst: h.indirect_dma_start(out=dst, out_offset=None, in_=gath.ap(),
                                                                          in_offset=bass.IndirectOffsetOnAxis(ap=pidx[:, k:k + 1], axis=0)),
                       reads=[gr_, cres], writes=[dst_res])
    eb_r = Res()
    sinkx = A("sinkx", [128, 16], F32); sink_r = Res()

    def build_eb(fd, L, bases, dst, dsts, nhead=16, h0=0):
        nb = len(bases)
        with ExitStack() as es:
            hk = [S(es, f"hk{i}", [128, nhead, 128], F32) for i in range(2)]; hr = [Res(), Res()]
            for bi, base in enumerate(bases):
                i = bi % 2
                src = bass.AP(fd.ap().tensor, fd.ap().offset + h0 * L + base, [[1, 128], [L, nhead], [1, 128]])
                fw.dma(SP, hk[i][:], src, reads=[fd_r], writes=[hr[i]])
                for h in range(nhead):
                    p, pr = PF()
                    fw.op(PE, lambda hh, h=h: hh.matmul(p[:, 0:128], lhsT=hk[i][:, h, :], rhs=antij[:], start=True, stop=True), reads=[hr[i], cres], writes=[pr])
                    fw.op(ACT, lambda hh, h=h: hh.activation(out=dst[:, h, bi * 128:(bi + 1) * 128], in_=p[:, 0:128], func=AF.Exp), reads=[pr], writes=[eb_r])
            for bi in range(nb):
                fw.op(DVE, lambda hh, bi=bi: hh.tensor_copy(out=dsts[:, :, bi * 16:(bi + 1) * 16], in_=dst[:, :, bi * 128:bi * 128 + 16]), reads=[eb_r], writes=[eb_r])
            fw.op(DVE, lambda hh: hh.memset(dst[0:64, :, 64:128], 0.0), writes=[eb_r])
            fw.op(DVE, lambda hh: hh.memset(dst[64:128, :, (nb - 1) * 128:(nb - 1) * 128 + 64], 0.0), writes=[eb_r])
            fw.barrier()

    fd_r = Res()

    def build_t5(ebt5, ebt5s):
        with ExitStack() as es:
            tab = S(es, "t5tab", [32, 16], F32); sel = S(es, "t5selS", [32, 384], F32); fsb = S(es, "t5f", [16, 384], F32); r = Res()
            with nc.allow_non_contiguous_dma(reason="tiny"):
                fw.dma(SP, tab[:], I.t5_table, writes=[r]); fw.dma(SP, sel[:], I.t5sel, writes=[r])
            p, pr = PF()
            fw.op(PE, lambda h: h.matmul(p[0:16, 0:384], lhsT=tab[:], rhs=sel[:], start=True, stop=True), reads=[r], writes=[pr])
            evac(DVE, fsb[:], p[0:16, 0:384], [pr], [r])
            fw.dma(SP, fd_t5.ap(), fsb[:], reads=[r], writes=[fd_r])
            fw.barrier()
        build_eb(fd_t5, 384, [-128 + 129, 0 + 129], ebt5, ebt5s)

    def build_band_fd(l):
        with ExitStack() as es:
            raw = S(es, "braw", [16, 513], F32); bt = S(es, "bt", [16, 768], F32); r = Res()
            fw.dma(SP, raw[:], I.band_rel_table[l], writes=[r])
            fw.op(DVE, lambda h: h.memset(bt[:, 0:384], 0.0), writes=[r])
            fw.op(DVE, lambda h: h.tensor_scalar(out=bt[:, 0:384], in0=bt[:, 0:384], scalar1=raw[:, 0:1], scalar2=None, op0=ALU.add), reads=[r], writes=[r])
            fw.op(DVE, lambda h: h.tensor_copy(out=bt[:, 384:768], in_=raw[:, 0:384]), reads=[r], writes=[r])
            fw.dma(SP, fd_band.ap(), bt[:], reads=[r], writes=[fd_r])
            fw.barrier()

    def load_sink(l):
        with nc.allow_non_contiguous_dma(reason="tiny"):
            fw.dma(SP, sinkx[:], bcast(I.swa_sink[l:l + 1, :], 16), writes=[sink_r])
        fw.op(ACT, lambda h: h.activation(out=sinkx[:], in_=sinkx[:], func=AF.Exp), reads=[sink_r], writes=[sink_r])

    PT = [A(f"PT{i}", [128, 640], BF16) for i in range(2)]; PT_r = [Res(), Res()]; pti = [0]
    rec = [A(f"rec{i}", [128, 128], F32) for i in range(2)]; rec_r = [Res(), Res()]

    def attn(q_ap, kblocks, scale, out_ap, ob, nq, reads, out_res, eb_all=None, sink_ap=None):
        i = pti[0]; pti[0] = 1 - i
        P = PT[i]; Pr = PT_r[i]
        nb = len(kblocks); per = max(1, 512 // nq)
        for g0 in range(0, nb, per):
            blks = kblocks[g0:g0 + per]
            p, pr = PF()
            for j, (kT, nk, v, on, eb, ro) in enumerate(blks):
                fw.op(PE, lambda h, j=j, kT=kT, nk=nk, ro=ro: h.matmul(p[ro:ro + nk, j * nq:(j + 1) * nq], lhsT=kT, rhs=q_ap, start=True, stop=True),
                      reads=reads, writes=[pr], inc=(j == len(blks) - 1))
            uniform = all(b[1] == 128 for b in blks) and (eb_all is not None or all(b[4] is None for b in blks))
            if uniform:
                w = len(blks) * nq
                fw.op(ACT, lambda h, w=w, g0=g0: h.activation(out=P[:, g0 * nq:g0 * nq + w], in_=p[:, 0:w], func=AF.Exp, scale=float(scale)), reads=[pr], writes=[Pr])
                if eb_all is not None:
                    fw.op(DVE, lambda h, w=w, g0=g0: h.tensor_tensor(out=P[:, g0 * nq:g0 * nq + w], in0=P[:, g0 * nq:g0 * nq + w], in1=eb_all[:, g0 * nq:g0 * nq + w], op=ALU.mult),
                          reads=[Pr, eb_r], writes=[Pr])
            if (not uniform) or eb_all is None:
                for j, (kT, nk, v, on, eb, ro) in enumerate(blks):
                    c0 = (g0 + j) * nq
                    if not uniform:
                        fw.op(ACT, lambda h, j=j, nk=nk, c0=c0, ro=ro: h.activation(out=P[ro:ro + nk, c0:c0 + nq], in_=p[ro:ro + nk, j * nq:(j + 1) * nq], func=AF.Exp, scale=float(scale)),
                              reads=[pr], writes=[Pr])
                    if eb is not None:
                        fw.op(DVE, lambda h, nk=nk, c0=c0, eb=eb, ro=ro: h.tensor_tensor(out=P[ro:ro + nk, c0:c0 + nq], in0=P[ro:ro + nk, c0:c0 + nq], in1=eb, op=ALU.mult),
                              reads=[Pr, eb_r], writes=[Pr])
        po, por = PF(); pd, pdr = PF()
        for j, (kT, nk, v, on, eb, ro) in enumerate(kblocks):
            fw.op(PE, lambda h, j=j, nk=nk, v=v, ro=ro: h.matmul(po[ob:ob + 64, 0:nq], lhsT=v, rhs=P[ro:ro + nk, j * nq:(j + 1) * nq], start=(j == 0), stop=(j == nb - 1)),
                  reads=[Pr] + reads, writes=[por], inc=(j == nb - 1))
        for j, (kT, nk, v, on, eb, ro) in enumerate(kblocks):
            fw.op(PE, lambda h, j=j, nk=nk, on=on, ro=ro: h.matmul(pd[ob:ob + 64, 0:nq], lhsT=on, rhs=P[ro:ro + nk, j * nq:(j + 1) * nq], start=(j == 0), stop=(j == nb - 1)),
                  reads=[Pr, cres], writes=[pdr], inc=(j == nb - 1))
        r = rec[i]; rr = rec_r[i]
        if sink_ap is not None:
            fw.op(DVE, lambda h: h.tensor_scalar(out=r[ob:ob + 64, 0:nq], in0=pd[ob:ob + 64, 0:nq], scalar1=sink_ap, scalar2=None, op0=ALU.add), reads=[pdr, sink_r], writes=[rr])
            fw.op(DVE, lambda h: h.reciprocal(out=r[ob:ob + 64, 0:nq], in_=r[ob:ob + 64, 0:nq]), reads=[rr], writes=[rr])
        else:
            fw.op(DVE, lambda h: h.reciprocal(out=r[ob:ob + 64, 0:nq], in_=pd[ob:ob + 64, 0:nq]), reads=[pdr], writes=[rr])
        fw.op(DVE, lambda h: h.tensor_tensor(out=out_ap, in0=po[ob:ob + 64, 0:nq], in1=r[ob:ob + 64, 0:nq], op=ALU.mult), reads=[por, rr], writes=[out_res])

    sg = [A(f"sg{i}", [128, 512], F32) for i in range(2)]; sg_r = [Res(), Res()]; sgi = [0]

    def merge_branch(l, b, gate_c0, wbr):
        for mb in range(4):
            Wg, Wgr = wload(I.w_in[l, :, gate_c0 + mb * 512:gate_c0 + (mb + 1) * 512], KC, 512)
            Wb, Wbr = wload(wbr[l, :, mb * 512:(mb + 1) * 512], 8, 512)
            for mi in range(4):
                m = mb * 4 + mi
                for (t0, n) in TOKBLK:
                    pg, pgr = PF()
                    for k in range(KC):
                        fw.op(PE, lambda h, k=k: h.matmul(pg[:, 0:n], lhsT=Wg[:, k, mi * 128:(mi + 1) * 128], rhs=xT[:, k, t0:t0 + n], start=(k == 0), stop=(k == KC - 1)),
                              reads=[Wgr] + xT_r, writes=[pgr], inc=(k == KC - 1))
                    pbr, pbrr = PF()
                    for k in range(8):
                        fw.op(PE, lambda h, k=k: h.matmul(pbr[:, 0:n], lhsT=Wb[:, k, mi * 128:(mi + 1) * 128], rhs=oT[:, k, t0:t0 + n], start=(k == 0), stop=(k == 7)),
                              reads=[Wbr, oT_r], writes=[pbrr], inc=(k == 7))
                    i = sgi[0]; sgi[0] = 1 - i
                    fw.op(ACT, lambda h: h.activation(out=sg[i][:, 0:n], in_=pg[:, 0:n], func=AF.Sigmoid), reads=[pgr], writes=[sg_r[i]])
                    if b == 0:
                        fw.op(DVE, lambda h: h.tensor_tensor(out=mixT[:, m, t0:t0 + n], in0=pbr[:, 0:n], in1=sg[i][:, 0:n], op=ALU.mult), reads=[pbrr, sg_r[i]], writes=[mix_r])
                    else:
                        fw.op(DVE, lambda h: h.tensor_tensor(out=sg[i][:, 0:n], in0=pbr[:, 0:n], in1=sg[i][:, 0:n], op=ALU.mult), reads=[pbrr, sg_r[i]], writes=[sg_r[i]])
                        fw.op(DVE, lambda h: h.tensor_tensor(out=mixT[:, m, t0:t0 + n], in0=sg[i][:, 0:n], in1=mixT[:, m, t0:t0 + n], op=ALU.add), reads=[sg_r[i], mix_r], writes=[mix_r])
        fw.barrier()
    def band_branch(l):
        build_band_fd(l)
        for hg in range(4):
            with ExitStack() as es:
                qcT = S(es, "qcT", [128, 2, T], BF16); kcT = S(es, "kcT", [128, 2, HALO + T], BF16); vc = S(es, "vc", [128, 13, 256], BF16)
                ebb = S(es, "ebb", [128, 4, 640], BF16); ebbs = S(es, "ebbs", [128, 4, 80], BF16); ebnb = S(es, "ebnb", [64, 4, 16], BF16)
                ckb = S(es, "ckb", [128, 2, 4, 256], BF16); ckbT = S(es, "ckbT", [128, 2, 2, 512], BF16); cvb = S(es, "cvb", [128, 2, 4, 256], BF16)
                qr = Res(); kr = Res(); vr = Res(); cr_ = Res(); ctr = Res()
                build_eb(fd_band, 768, [1, 129, 257, 385, 513], ebb, ebbs, nhead=4, h0=4 * hg)
                with nc.allow_non_contiguous_dma(reason="tiny"):
                    fw.dma(SP, ebnb[0:16, :, :], ebbs[0:16, :, 64:80], reads=[eb_r], writes=[eb_r])
                    fw.dma(SP, ebnb[32:48, :, :], ebbs[0:16, :, 64:80], reads=[eb_r], writes=[eb_r])
                c0 = hg * 256
                W, Wr = wload(I.w_in[l, :, C_QC + c0:C_QC + c0 + 256], KC, 256)
                for ch in range(2):
                    def cons(p, pr, t0, n, ch=ch):
                        evac(ACT if ch else DVE, qcT[:, ch, t0:t0 + n], p[:, 0:n], [pr], [qr])
                    proj_ws(W, Wr, KC, ch * 128, 128, xsrc, TOKBLK, cons, xT_r)
                W, Wr = wload(I.w_in[l, :, C_KC + c0:C_KC + c0 + 256], KC, 256)
                for ch in range(2):
                    def consh(p, pr, t0, n, ch=ch):
                        evac(DVE, kcT[:, ch, 0:512], p[:, 0:512], [pr], [kr])
                    proj_ws(W, Wr, KC, ch * 128, 128, lambda k, t0, n: XH['t'][:, k, 0:512], [(0, 512)], consh, [xh_r])
                    def consl(p, pr, t0, n, ch=ch):
                        evac(ACT, kcT[:, ch, 512 + t0:512 + t0 + n], p[:, 0:n], [pr], [kr])
                    proj_ws(W, Wr, KC, ch * 128, 128, xsrc, TOKBLK, consl, xT_r)
                for tt in [4, 5, 6, 7, 8]:
                    nt = NT[tt]
                    def consk(p, pr, tt=tt, nt=nt):
                        s, sr = STG()
                        evac(DVE, s[0:nt, 0:256], p[0:nt, 0:256], [pr], [sr])
                        dst = o_band_p[l, 0, (tt - 4) * 128:(tt - 3) * 128, c0:c0 + 256] if tt < 8 else o_band_s[l, 0, :, c0:c0 + 256]
                        fw.dma(SP, dst, s[0:nt, 0:256], reads=[sr], is_out=True)
                    proj_as(W, Wr, KC, 0, 256, lambda k, tt=tt, nt=nt: xT[:, k, TOK0[tt]:TOK0[tt] + nt], nt, consk, [xT_r[tt]])
                W, Wr = wload(I.w_in[l, :, C_VC + c0:C_VC + c0 + 256], KC, 256)
                for slot in range(13):
                    if slot < 4:
                        lhs = lambda k, slot=slot: XH['t'][:, k, slot * 128:(slot + 1) * 128]; nt = 128; lr = [xh_r]
                    else:
                        tt = slot - 4; nt = NT[tt]
                        lhs = lambda k, tt=tt, nt=nt: xT[:, k, TOK0[tt]:TOK0[tt] + nt]; lr = [xT_r[tt]]
                    def consv(p, pr, slot=slot, nt=nt):
                        if slot < 4:
                            fw.op(DVE, lambda h: h.tensor_scalar(out=vc[0:nt, slot, :], in0=p[0:nt, 0:256], scalar1=flag[0:nt, 0:1], scalar2=None, op0=ALU.mult),
                                  reads=[pr, cres], writes=[vr])
                        else:
                            evac(DVE, vc[0:nt, slot, :], p[0:nt, 0:256], [pr], [vr])
                        if slot >= 8:
                            s, sr = STG()
                            evac(DVE, s[0:nt, 0:256], p[0:nt, 0:256], [pr], [sr])
                            dst = o_band_p[l, 1, (slot - 8) * 128:(slot - 7) * 128, c0:c0 + 256] if slot < 12 else o_band_s[l, 1, :, c0:c0 + 256]
                            fw.dma(SP, dst, s[0:nt, 0:256], reads=[sr], is_out=True)
                    proj_as(W, Wr, KC, 0, 256, lhs, nt, consv, lr)
                for b in range(2):
                    fw.dma(POOL, ckb[:, b], I.c_band_k[l, b, :, c0:c0 + 256].rearrange("(t p) f -> p t f", p=128), writes=[cr_])
                    fw.dma(POOL, cvb[:, b], I.c_band_v[l, b, :, c0:c0 + 256].rearrange("(t p) f -> p t f", p=128), writes=[cr_])
                    for t in range(4):
                        transpose_into(lambda j, b=b, t=t: ckbT[:, b, j, t * 128:(t + 1) * 128], ckb[:, b, t, :], cr_, 128, 256, ctr)
                for hl in range(4):
                    h = 4 * hg + hl; ch = hl // 2; ob = (h % 2) * 64; ps_ = slice(ob, ob + 64); vs = slice(hl * 64, hl * 64 + 64)
                    for qt in range(8):
                        kb_ = [(kcT[ps_, ch, (qt + j) * 128:(qt + j + 1) * 128], 128, vc[:, qt + j, vs], ones_h[:] if qt + j < 4 else ones_b[:], None, 0) for j in range(5)]
                        attn(qcT[ps_, ch, qt * 128:(qt + 1) * 128], kb_, 0.125, oT[ps_, h // 2, qt * 128:(qt + 1) * 128], ob, 128, [qr, kr, vr], oT_r, eb_all=ebb[:, hl, :])
                    for b in range(2):
                        q0 = TP + 32 * b; rs = slice(32 * b, 32 * b + 16)
                        kb_ = [(ckbT[ps_, b, ch, j * 128:(j + 1) * 128], 128, cvb[:, b, j, vs], ones_b[:], ebbs[:, hl, j * 16:(j + 1) * 16], 0) for j in range(4)]
                        kb_.append((kcT[ps_, ch, 512 + q0:512 + q0 + 16], 16, vc[rs, 12, vs], ones_b[rs, :], ebnb[rs, hl, :], 32 * b))
                        attn(qcT[ps_, ch, q0:q0 + 16], kb_, 0.125, oT[ps_, h // 2, q0:q0 + 16], ob, 16, [qr, kr, vr, cr_, ctr], oT_r)
                fw.barrier()
        merge_branch(l, 2, C_GC, I.w_br_c)
    SIM = bool(os.environ.get("MK_SIM"))
    G64 = [math.exp(LOGG[h] * 64) for h in range(8)]

    def ret_branch(l):
        RSUB = int(os.environ.get("MK_RSUB", "99"))
        with ExitStack() as es:
            qsT = S(es, "qsT", [128, 4, T], BF16); kbT = S(es, "kbT", [128, 4, T], BF16)
            kbw = S(es, "kbw", [128, 9, 512], BF16); vb = S(es, "vb", [128, 9, 1024], BF16)
            rope = S(es, "rope", [128, 9, 64], F32); decT = S(es, "decT", [128, 512], F32); decTs = S(es, "decTs", [64, 128], F32)
            g64x = S(es, "g64x", [128, 512], F32); g16x = S(es, "g16x", [128, 512], F32)
            Sf = S(es, "Sf", [128, 512], F32); Sb = S(es, "Sb", [128, 512], BF16); s_r = Res(); sb_r = Res()
            kc_r = Res(); q_r = Res(); k_r = Res(); kw_r = Res(); v_r = Res()
            with nc.allow_non_contiguous_dma(reason="consts"):
                for t_, src in ((rope, I.rope), (decT, I.decT), (decTs, I.decTs), (g64x, I.g64x), (g16x, I.g16x)):
                    fw.dma(SP, t_[:], src, writes=[kc_r])
            Wq, Wqr = wload(I.w_in[l, :, C_QB:C_QB + 512], KC, 512)
            Wk, Wkr = wload(I.w_in[l, :, C_KB:C_KB + 512], KC, 512)
            with ExitStack() as es2:
                ta = S(es2, "ropeA", [128, 8, 32], F32); tb = S(es2, "ropeB", [128, 8, 32], F32); rq = S(es2, "rq", [128, 8, 64], F32); t_r = Res()
                fx = [S(es2, f"fx{i}", [128, 512], F32) for i in range(2)]; fx_r = [Res(), Res()]
                rb16 = [S(es2, f"rb16{i}", [128, 512], BF16) for i in range(2)]; rb_r = [Res(), Res()]
                cnt = [0]
                for tt in range(9):
                    nt = NT[tt]; cols = slice(TOK0[tt], TOK0[tt] + nt)
                    for which, (W, Wr) in enumerate(((Wq, Wqr), (Wk, Wkr))):
                        sc = 1.0 if which == 0 else 0.125
                        def cons(p, pr, tt=tt, nt=nt, which=which, sc=sc, cols=cols):
                            pv = p[0:nt, 0:512].rearrange("p (h two d) -> p h two d", two=2, d=32)
                            x1 = pv[:, :, 0, :]; x2 = pv[:, :, 1, :]
                            ra = rope[0:nt, tt, 0:32]; rs_ = rope[0:nt, tt, 32:64]
                            cosb = bass.AP(ra.tensor, ra.offset, [list(ra.ap[0]), [0, 8], [1, 32]])
                            sinb = bass.AP(rs_.tensor, rs_.offset, [list(rs_.ap[0]), [0, 8], [1, 32]])
                            M = ALU.mult
                            fw.op(DVE, lambda h: h.scalar_tensor_tensor(out=ta[0:nt], in0=x1, scalar=sc, in1=cosb, op0=M, op1=M), reads=[pr, kc_r], writes=[t_r])
                            fw.op(DVE, lambda h: h.scalar_tensor_tensor(out=tb[0:nt], in0=x2, scalar=sc, in1=sinb, op0=M, op1=M), reads=[pr, kc_r], writes=[t_r])
                            fw.op(DVE, lambda h: h.tensor_tensor(out=rq[0:nt, :, 0:32], in0=ta[0:nt], in1=tb[0:nt], op=ALU.subtract), reads=[t_r], writes=[t_r])
                            fw.op(DVE, lambda h: h.scalar_tensor_tensor(out=ta[0:nt], in0=x2, scalar=sc, in1=cosb, op0=M, op1=M), reads=[pr, kc_r], writes=[t_r])
                            fw.op(DVE, lambda h: h.scalar_tensor_tensor(out=tb[0:nt], in0=x1, scalar=sc, in1=sinb, op0=M, op1=M), reads=[pr, kc_r], writes=[t_r])
                            fw.op(DVE, lambda h: h.tensor_tensor(out=rq[0:nt, :, 32:64], in0=ta[0:nt], in1=tb[0:nt], op=ALU.add), reads=[t_r], writes=[t_r])
                            i = cnt[0] % 2; cnt[0] += 1
                            rqf = rq[0:nt].rearrange("p h d -> p (h d)")
                            fw.dma(SP, fx[i][0:nt, :], (I.crossx if which == 0 else I.kwx)[0:nt, tt, :], writes=[fx_r[i]])
                            if which == 0:
                                fw.op(DVE, lambda h: h.tensor_tensor(out=rb16[i][0:nt, :], in0=rqf, in1=fx[i][0:nt, :], op=ALU.mult), reads=[t_r, fx_r[i]], writes=[rb_r[i]])
                                transpose_into(lambda j: qsT[:, j, cols], rb16[i], rb_r[i], nt, 512, q_r)
                            else:
                                fw.op(ACT, lambda h: h.copy(out=rb16[i][0:nt, :], in_=rqf), reads=[t_r], writes=[rb_r[i]])
                                fw.op(DVE, lambda h: h.tensor_tensor(out=kbw[0:nt, tt, :], in0=rqf, in1=fx[i][0:nt, :], op=ALU.mult), reads=[t_r, fx_r[i]], writes=[kw_r])
                                transpose_into(lambda j: kbT[:, j, cols], rb16[i], rb_r[i], nt, 512, k_r)
                        proj_as(W, Wr, KC, 0, 512, lambda k, cols=cols: xT[:, k, cols], nt, cons, [xT_r[tt]])
                fw.barrier()
            if RSUB == 0:
                fw.barrier(); return
            for half in range(2):
                W, Wr = wload(I.w_in[l, :, C_VB + half * 512:C_VB + (half + 1) * 512], KC, 512)
                for tt in range(9):
                    nt = NT[tt]
                    def consv(p, pr, tt=tt, nt=nt, half=half):
                        evac(ACT if tt % 2 else DVE, vb[0:nt, tt, half * 512:(half + 1) * 512], p[0:nt, 0:512], [pr], [v_r])
                    proj_as(W, Wr, KC, 0, 512, lambda k, tt=tt, nt=nt: xT[:, k, TOK0[tt]:TOK0[tt] + nt], nt, consv, [xT_r[tt]])
            for half in range(2):
                W, Wr = wload(I.w_in[l, :, C_GR + half * 512:C_GR + (half + 1) * 512], KC, 512)
                for mi in range(4):
                    def consg(p, pr, t0, n, m=half * 4 + mi):
                        fw.op(ACT, lambda h: h.activation(out=oT[:, m, t0:t0 + n], in_=p[:, 0:n], func=AF.Silu), reads=[pr], writes=[oT_r])
                    proj_ws(W, Wr, KC, mi * 128, 128, xsrc, TOKBLK, consg, xT_r)

            if RSUB == 1:
                fw.barrier(); return
            def state_update(tt, rb, nk, gx, Sx, Sx_r, pu_pair=None):
                pu, pur = pu_pair if pu_pair is not None else PF()
                for h in range(8):
                    hp = (h % 2) * 64
                    fw.op(PE, lambda hh, h=h, hp=hp: hh.matmul(pu[hp:hp + 64, (h // 2) * 128:(h // 2 + 1) * 128], lhsT=kbw[rb:rb + nk, tt, h * 64:(h + 1) * 64],
                                                               rhs=vb[rb:rb + nk, tt, h * 128:(h + 1) * 128], start=True, stop=True),
                          reads=[kw_r, v_r], writes=[pur], inc=(h == 7))
                fw.op(DVE, lambda hh: hh.tensor_tensor(out=Sx[:], in0=Sx[:], in1=gx[:], op=ALU.mult), reads=[Sx_r, kc_r], writes=[Sx_r])
                fw.op(DVE, lambda hh: hh.tensor_tensor(out=Sx[:], in0=Sx[:], in1=pu[:, :], op=ALU.add), reads=[Sx_r, pur], writes=[Sx_r])

            fw.op(DVE, lambda hh: hh.memset(Sf[:], 0.0), writes=[s_r])
            for c in range(16):
                state_update(c // 2, (c % 2) * 64, 64, g64x, Sf, s_r)
            if RSUB == 2:
                fw.barrier(); return
            with ExitStack() as es2:
                up = [S(es2, f"up{k}", [128, 512], F32) for k in range(3)]; up_r = Res()
                rco = S(es2, "rco", [128, 3, 512], F32)
                fw.dma(SP, rco[:], I.rcoef, writes=[kc_r])
                if not SIM:
                    exchange(Sf[:], s_r, ex_u_b, ex_u_g, 512, [u[:] for u in up], up_r)
                else:
                    for u in up:
                        fw.op(DVE, lambda hh, u=u: hh.memset(u[:], 0.0), writes=[up_r])
                fw.op(DVE, lambda hh: hh.tensor_tensor(out=Sf[:], in0=up[0][:], in1=rco[:, 0, :], op=ALU.mult), reads=[up_r, kc_r, s_r], writes=[s_r])
                for k in (1, 2):
                    fw.op(DVE, lambda hh, k=k: hh.tensor_tensor(out=up[k][:], in0=up[k][:], in1=rco[:, k, :], op=ALU.mult), reads=[up_r, kc_r], writes=[up_r])
                    fw.op(DVE, lambda hh, k=k: hh.tensor_tensor(out=Sf[:], in0=Sf[:], in1=up[k][:], op=ALU.add), reads=[up_r, s_r], writes=[s_r])
                fw.barrier()

            if RSUB == 3:
                fw.barrier(); return
            with ExitStack() as es2:
                AT = [S(es2, f"AT{i}", [128, 512], BF16) for i in range(2)]; AT_r = [Res(), Res()]
                sq = S(es2, "sq", [128, 1024], F32); sq_r = Res()
                onb = [S(es2, f"onb{i}", [128, 1024], BF16) for i in range(2)]; on_r = [Res(), Res()]
                st = [S(es2, f"gst{i}", [128, 40], F32) for i in range(2)]; st_r = [Res(), Res()]
                Ss = S(es2, "Ss", [128, 512], F32); Ssb = S(es2, "Ssb", [128, 512], BF16); ss_r = Res(); ssb_r = Res()
                ofs = S(es2, "ofs", [128, 1024], F32); of_r = Res()

                def chunk_out(tt, rb, nk, cols, dec_ap, Sb_, Sb_r, po2, po2_r, atbuf, at_r, pc2, pc2_r, ps_pair):
                    ps, psr = ps_pair

                    def pe_fence():
                        PE.h.wait_ge(PE.sem, PE.count)
                    for par in range(2):
                        hs = [par, par + 2, par + 4, par + 6]; hp = par * 64
                        for h in hs:
                            fw.op(PE, lambda hh, h=h, hp=hp: hh.matmul(ps[rb:rb + nk, h * nk:(h + 1) * nk], lhsT=kbT[hp:hp + 64, h // 2, cols], rhs=qsT[hp:hp + 64, h // 2, cols],
                                                                       start=True, stop=True), reads=[k_r, q_r], writes=[psr], inc=(h == hs[-1]))
                        pe_fence()
                    fw.op(DVE, lambda hh: hh.tensor_tensor(out=atbuf[rb:rb + nk, 0:8 * nk], in0=ps[rb:rb + nk, 0:8 * nk], in1=dec_ap, op=ALU.mult), reads=[psr, kc_r], writes=[at_r])
                    if RSUB == 10:
                        return
                    for h in range(8):
                        pq = po2[h // 4]; pqr = po2_r[h // 4]; oc = (h % 4) * 128
                        fw.op(PE, lambda hh, h=h, pq=pq, oc=oc: hh.matmul(pq[rb:rb + nk, oc:oc + 128], lhsT=atbuf[rb:rb + nk, h * nk:(h + 1) * nk], rhs=vb[rb:rb + nk, tt, h * 128:(h + 1) * 128],
                                                                         start=True, stop=True), reads=[at_r, v_r], writes=[pqr], inc=(h % 4 == 3))
                    for par in range(2):
                        hs = [par, par + 2, par + 4, par + 6]; hp = par * 64
                        for h in hs:
                            pc = pc2[h // 4]; pcr = pc2_r[h // 4]; oc = (h % 4) * 128
                            fw.op(PE, lambda hh, h=h, hp=hp, pc=pc, oc=oc: hh.matmul(pc[rb:rb + nk, oc:oc + 128], lhsT=qsT[hp:hp + 64, h // 2, cols], rhs=Sb_[hp:hp + 64, (h // 2) * 128:(h // 2 + 1) * 128],
                                                                                    start=True, stop=True), reads=[q_r, Sb_r], writes=[pcr], inc=True)
                        pe_fence()

                def finish_rows(tt, r0, nr, po2, po2_r, i, pc2, pc2_r):
                    s = st[i]; rs = slice(r0, r0 + nr)
                    for hb in range(2):
                        sl = slice(hb * 512, (hb + 1) * 512)
                        fw.op(DVE, lambda hh, hb=hb, sl=sl: hh.tensor_copy(out=ofs[rs, sl], in_=po2[hb][rs, :]), reads=[po2_r[hb]], writes=[of_r])
                        fw.op(DVE, lambda hh, hb=hb, sl=sl: hh.tensor_tensor(out=ofs[rs, sl], in0=ofs[rs, sl], in1=pc2[hb][rs, :], op=ALU.add), reads=[of_r, pc2_r[hb]], writes=[of_r])
                        fw.op(DVE, lambda hh, hb=hb, sl=sl: hh.tensor_reduce(out=s[rs, hb * 4:(hb + 1) * 4], in_=ofs[rs, sl].rearrange("p (h d) -> p h d", d=128), axis=AX.X, op=ALU.add),
                              reads=[of_r], writes=[st_r[i]])
                        fw.op(DVE, lambda hh, sl=sl: hh.tensor_tensor(out=sq[rs, sl], in0=ofs[rs, sl], in1=ofs[rs, sl], op=ALU.mult), reads=[of_r], writes=[sq_r])
                    fw.op(DVE, lambda hh: hh.tensor_reduce(out=s[rs, 8:16], in_=sq[rs, :].rearrange("p (h d) -> p h d", d=128), axis=AX.X, op=ALU.add), reads=[sq_r], writes=[st_r[i]])
                    fw.op(DVE, lambda hh: hh.tensor_scalar(out=s[rs, 16:24], in0=s[rs, 0:8], scalar1=1.0 / 128, scalar2=None, op0=ALU.mult), reads=[st_r[i]], writes=[st_r[i]])
                    fw.op(DVE, lambda hh: hh.tensor_tensor(out=s[rs, 24:32], in0=s[rs, 16:24], in1=s[rs, 16:24], op=ALU.mult), reads=[st_r[i]], writes=[st_r[i]])
                    fw.op(DVE, lambda hh: hh.scalar_tensor_tensor(out=s[rs, 32:40], in0=s[rs, 8:16], scalar=1.0 / 128, in1=s[rs, 24:32], op0=ALU.mult, op1=ALU.subtract), reads=[st_r[i]], writes=[st_r[i]])
                    fw.op(DVE, lambda hh: hh.tensor_scalar(out=s[rs, 32:40], in0=s[rs, 32:40], scalar1=1e-5, scalar2=None, op0=ALU.add), reads=[st_r[i]], writes=[st_r[i]])
                    fw.op(ACT, lambda hh: hh.activation(out=s[rs, 32:40], in_=s[rs, 32:40], func=AF.Sqrt), reads=[st_r[i]], writes=[st_r[i]])
                    fw.op(DVE, lambda hh: hh.reciprocal(out=s[rs, 32:40], in_=s[rs, 32:40]), reads=[st_r[i]], writes=[st_r[i]])
                    for h in range(8):
                        fw.op(DVE, lambda hh, h=h: hh.tensor_scalar(out=onb[i][rs, h * 128:(h + 1) * 128], in0=ofs[rs, h * 128:(h + 1) * 128],
                                                                    scalar1=s[rs, 16 + h:17 + h], scalar2=s[rs, 32 + h:33 + h], op0=ALU.subtract, op1=ALU.mult),
                              reads=[of_r, st_r[i]], writes=[on_r[i]])

                def to_oT(tt, nt, i):
                    cols = slice(TOK0[tt], TOK0[tt] + nt)
                    def ev(j, pT, pr):
                        fw.op(DVE, lambda hh: hh.tensor_tensor(out=oT[:, j, cols], in0=pT, in1=oT[:, j, cols], op=ALU.mult), reads=[pr, oT_r], writes=[oT_r])
                    transpose_into(None, onb[i], on_r[i], nt, 1024, oT_r, evac_fn=ev)

                for tt in range(8 if RSUB >= 20 or RSUB == 4 else 1):
                    i = tt % 2
                    po2 = []; po2_r = []; pc2 = []; pc2_r = []
                    for _ in range(2):
                        p, pr = PF(); po2.append(p); po2_r.append(pr)
                    for _ in range(2):
                        p, pr = PF(); pc2.append(p); pc2_r.append(pr)
                    psp = PF(); pup = PF()
                    for cc in range(2):
                        c = 2 * tt + cc; rb = cc * 64
                        fw.op(ACT, lambda hh: hh.copy(out=Sb[:], in_=Sf[:]), reads=[s_r], writes=[sb_r])
                        chunk_out(tt, rb, 64, slice(c * 64, (c + 1) * 64), decT[rb:rb + 64, :], Sb, sb_r, po2, po2_r, AT[cc], AT_r[cc], pc2, pc2_r, psp)
                        if RSUB in (10, 11):
                            continue
                        state_update(tt, rb, 64, g64x, Sf, s_r, pup)
                    if RSUB in (10, 11, 12):
                        break
                    finish_rows(tt, 0, 128, po2, po2_r, i, pc2, pc2_r)
                    if RSUB == 13:
                        break
                    to_oT(tt, 128, i)
                if RSUB in (10, 11, 12, 13, 14):
                    fw.barrier(); return
                if RSUB == 4:
                    fw.barrier(); return
                for two in range(2):
                    fw.dma(SP, o_ret_p[l].rearrange("(j two) k v -> two k j v", two=2)[two], Sf[two * 64:(two + 1) * 64, :].rearrange("p (j v) -> p j v", v=128),
                           reads=[s_r], is_out=True)
                i = 0
                po2 = []; po2_r = []; pc2 = []; pc2_r = []
                for _ in range(2):
                    p, pr = PF(); po2.append(p); po2_r.append(pr)
                for _ in range(2):
                    p, pr = PF(); pc2.append(p); pc2_r.append(pr)
                psp = PF(); pup = PF()
                fw.op(DVE, lambda hh: hh.memset(onb[i][0:TS, :], 0.0), writes=[on_r[i]])
                for b in range(2):
                    rb = 32 * b
                    for two in range(2):
                        fw.dma(SP, Ss[two * 64:(two + 1) * 64, :].rearrange("p (j v) -> p j v", v=128), I.c_ret[l, b].rearrange("(j two) k v -> two k j v", two=2)[two], writes=[ss_r])
                    fw.op(ACT, lambda hh: hh.copy(out=Ssb[:], in_=Ss[:]), reads=[ss_r], writes=[ssb_r])
                    chunk_out(8, rb, 16, slice(TP + rb, TP + rb + 16), decTs[rb:rb + 16, :], Ssb, ssb_r, po2, po2_r, AT[b], AT_r[b], pc2, pc2_r, psp)
                    state_update(8, rb, 16, g16x, Ss, ss_r, pup)
                    for two in range(2):
                        fw.dma(SP, o_ret_s[l, b].rearrange("(j two) k v -> two k j v", two=2)[two], Ss[two * 64:(two + 1) * 64, :].rearrange("p (j v) -> p j v", v=128),
                               reads=[ss_r], writes=[ss_r], is_out=True)
                    finish_rows(8, rb, 16, po2, po2_r, i, pc2, pc2_r)
                to_oT(8, TS, i)
                fw.barrier()
        if RSUB == 5:
            return
        merge_branch(l, 1, C_GB, I.w_br_b)
    def cross_attn(l, caT):
        ca_r = mix_r
        with ExitStack() as es:
            memT = S(es, "memT", [128, KC, 256], BF16)
            mtm = S(es, "mtm", [128, 2, D], BF16); mr = Res()
            fw.dma(POOL, mtm[:], I.mem.rearrange("(t p) d -> p t d", p=128), writes=[mr])
            for t in range(2):
                transpose_into(lambda j, t=t: memT[:, j, t * 128:(t + 1) * 128], mtm[:, t, :], mr, 128, D, memT_r)
            mkh = S(es, "mkh", [128, 2, 512], BF16); mkhT = S(es, "mkhT", [128, 4, 256], BF16); mvh = S(es, "mvh", [128, 2, 512], BF16)
            qTh = S(es, "qTh", [128, 4, T], BF16)
            Pk = [S(es, f"Pk{i}", [128, 512], BF16) for i in range(2)]; rc = S(es, "rc", [128, 512], F32)
            ones128 = S(es, "ones128", [128, 128], BF16)
            ck = S(es, "cmk", [128, 2, 512], BF16); ckT = S(es, "cmkT", [128, 4, 256], BF16); cv = S(es, "cmv", [128, 2, 512], BF16)
            mk_r = Res(); mkT_r = Res(); mv_r = Res(); q_r = Res(); P_r = [Res(), Res()]; rc_r = Res(); c_r = Res(); cT_r = Res()
            fw.op(DVE, lambda h: h.memset(ones128[:], 1.0), writes=[cres])
            SC = 512 ** -0.5

            def block(hd, t0, n, kT_fn, kT_res, v_fn, v_res):
                for kt in range(2):
                    ps, psr = PF()
                    for dc in range(4):
                        fw.op(PE, lambda h, dc=dc: h.matmul(ps[:, 0:n], lhsT=kT_fn(dc, kt), rhs=qTh[:, dc, t0:t0 + n], start=(dc == 0), stop=(dc == 3)),
                              reads=[q_r] + kT_res, writes=[psr], inc=(dc == 3))
                    fw.op(ACT, lambda h: h.activation(out=Pk[kt][:, 0:n], in_=ps[:, 0:n], func=AF.Exp, scale=float(SC)), reads=[psr], writes=[P_r[kt]])
                pd, pdr = PF()
                for kt in range(2):
                    fw.op(PE, lambda h, kt=kt: h.matmul(pd[:, 0:n], lhsT=ones128[:], rhs=Pk[kt][:, 0:n], start=(kt == 0), stop=(kt == 1)), reads=[P_r[kt], cres], writes=[pdr], inc=(kt == 1))
                fw.op(DVE, lambda h: h.reciprocal(out=rc[:, 0:n], in_=pd[:, 0:n]), reads=[pdr], writes=[rc_r])
                for dc in range(4):
                    po, por = PF()
                    for kt in range(2):
                        fw.op(PE, lambda h, kt=kt, dc=dc: h.matmul(po[:, 0:n], lhsT=v_fn(kt, dc), rhs=Pk[kt][:, 0:n], start=(kt == 0), stop=(kt == 1)),
                              reads=[P_r[kt]] + v_res, writes=[por], inc=(kt == 1))
                    fw.op(DVE, lambda h, dc=dc: h.tensor_tensor(out=caT[:, hd * 4 + dc, t0:t0 + n], in0=po[:, 0:n], in1=rc[:, 0:n], op=ALU.mult), reads=[por, rc_r], writes=[ca_r])

            for hd in range(4):
                hc = slice(hd * 512, (hd + 1) * 512)
                for which, (Wd, dstb, dr) in enumerate(((I.w_mk, mkh, mk_r), (I.w_mv, mvh, mv_r))):
                    W, Wr = wload(Wd[l, :, hc], KC, 512)
                    for kt in range(2):
                        def cons(p, pr, kt=kt, which=which, dstb=dstb, dr=dr):
                            s, sr = STG()
                            evac(DVE, s[:, :], p[:, :], [pr], [sr])
                            fw.dma(SP, o_mem_p[l, which, kt * 128:(kt + 1) * 128, hc], s[:, :], reads=[sr], is_out=True)
                            evac(DVE, dstb[:, kt, :], p[:, :], [pr], [dr])
                        proj_as(W, Wr, KC, 0, 512, lambda k, kt=kt: memT[:, k, kt * 128:(kt + 1) * 128], 128, cons, [memT_r])
                for kt in range(2):
                    transpose_into(lambda j, kt=kt: mkhT[:, j, kt * 128:(kt + 1) * 128], mkh[:, kt, :], mk_r, 128, 512, mkT_r)
                W, Wr = wload(I.w_mq[l, :, hc], KC, 512)
                for dc in range(4):
                    def consq(p, pr, t0, n, dc=dc):
                        evac(ACT if dc % 2 else DVE, qTh[:, dc, t0:t0 + n], p[:, 0:n], [pr], [q_r])
                    proj_ws(W, Wr, KC, dc * 128, 128, xsrc, TOKBLK, consq, xT_r)
                for (t0, n) in [(0, 512), (512, 512)]:
                    block(hd, t0, n, lambda dc, kt: mkhT[:, dc, kt * 128:(kt + 1) * 128], [mkT_r], lambda kt, dc: mvh[:, kt, dc * 128:(dc + 1) * 128], [mv_r])
                for b in range(2):
                    fw.dma(POOL, ck[:], I.c_mem_k[l, b, :, hc].rearrange("(t p) f -> p t f", p=128), writes=[c_r])
                    fw.dma(POOL, cv[:], I.c_mem_v[l, b, :, hc].rearrange("(t p) f -> p t f", p=128), writes=[c_r])
                    for kt in range(2):
                        transpose_into(lambda j, kt=kt: ckT[:, j, kt * 128:(kt + 1) * 128], ck[:, kt, :], c_r, 128, 512, cT_r)
                    block(hd, TP + 32 * b, 16, lambda dc, kt: ckT[:, dc, kt * 128:(kt + 1) * 128], [cT_r], lambda kt, dc: cv[:, kt, dc * 128:(dc + 1) * 128], [c_r])
            fw.barrier()
    ex_r = Res()

    def ffn(l, x2h, x2h_r):
        pass

    def ffn_layer(l):
        with ExitStack() as es:
            hT = S(es, "hT", [128, 44, T], BF16); h_r = Res()
            cw = S(es, "cw", [128, 88, 3], F32); cb = S(es, "cb", [128, 88], F32); cst = S(es, "cst", [128, 88, 2, 2], F32); w_r = Res()
            x2h = S(es, "x2h", [128, KC, 2], BF16); x2h_r = Res()
            ufin = S(es, "ufin", [128, 88, 2], F32); ufs = S(es, "ufs", [128, 88, 2, 2], F32); uf_r = Res()
            with nc.allow_non_contiguous_dma(reason="per-channel conv params / conv state (feature-major)"):
                for j in range(3):
                    fw.dma(SP, cw[:, :, j], I.ffn_conv_w[l, j].rearrange("(c p) -> p c", p=128), writes=[w_r])
                fw.dma(SP, cb[:], I.ffn_conv_b[l].rearrange("(c p) -> p c", p=128), writes=[w_r])
                for b in range(2):
                    for t in range(2):
                        fw.dma(SP, cst[:, :, b, t], I.c_conv[l, b, t].rearrange("(c p) -> p c", p=128), writes=[w_r])
            x2s = S(es, "x2s", [128, KC, 2], BF16); x2s_r = Res()
            fw.op(DVE, lambda h: h.tensor_copy(out=x2s[:], in_=xT[:, :, TP - 2:TP]), reads=[xT_r[7]], writes=[x2s_r])
            if not SIM:
                exchange(x2s[:].rearrange("p k t -> p (k t)"), x2s_r, ex_c_b, ex_c_g, 32, [x2h[:].rearrange("p k t -> p (k t)")], x2h_r)
            else:
                fw.op(DVE, lambda h: h.memset(x2h[:], 0.0), writes=[x2h_r])
            uS = [S(es, f"uS{i}", [128, 2 + TP], F32) for i in range(2)]; uS_r = [Res(), Res()]
            uSs = [S(es, f"uSs{i}", [128, 2, 18], F32) for i in range(2)]; uSs_r = [Res(), Res()]
            cg = [S(es, f"cg{i}", [128, T], F32) for i in range(2)]; cg_r = [Res(), Res()]
            for blk in range(11):
                Wg, Wgr = wload(I.w_ffn_in[l, :, blk * 512:(blk + 1) * 512], KC, 512)
                Wv, Wvr = wload(I.w_ffn_in[l, :, DFF + blk * 512:DFF + (blk + 1) * 512], KC, 512)
                for mi in range(4):
                    c = blk * 4 + mi
                    for half, (W, Wr) in enumerate(((Wg, Wgr), (Wv, Wvr))):
                        ch = c + 44 * half; u = uS[half]; ur = uS_r[half]; us = uSs[half]; usr = uSs_r[half]
                        def ch_(p, pr, t0, n, u=u, ur=ur):
                            fw.op(DVE, lambda h: h.tensor_scalar(out=u[:, 0:2], in0=p[:, 0:2], scalar1=flag[:, 0:1], scalar2=None, op0=ALU.mult), reads=[pr, cres], writes=[ur])
                        proj_ws(W, Wr, KC, mi * 128, 128, lambda k, t0, n: x2h[:, k, 0:2], [(0, 2)], ch_, [x2h_r])
                        def cl_(p, pr, t0, n, u=u, ur=ur, us=us, usr=usr, ch=ch):
                            if t0 < TP:
                                evac(DVE, u[:, 2 + t0:2 + t0 + n], p[:, 0:n], [pr], [ur])
                            else:
                                for b in range(2):
                                    evac(DVE, us[:, b, 2:18], p[:, 32 * b:32 * b + 16], [pr], [usr])
                                fw.op(DVE, lambda h: h.tensor_copy(out=us[:, :, 0:2], in_=cst[:, ch, :, :]), reads=[w_r], writes=[usr])
                        proj_ws(W, Wr, KC, mi * 128, 128, xsrc, TOKBLK, cl_, xT_r)
                        fw.op(DVE, lambda h, u=u, ch=ch: h.tensor_copy(out=ufin[:, ch, :], in_=u[:, TP:TP + 2]), reads=[ur], writes=[uf_r])
                        fw.op(DVE, lambda h, us=us, ch=ch: h.tensor_copy(out=ufs[:, ch, :, :], in_=us[:, :, 16:18]), reads=[usr], writes=[uf_r])
                        dst = cg[half]; dr = cg_r[half]
                        w0 = cw[:, ch, 0:1]; w1 = cw[:, ch, 1:2]; w2 = cw[:, ch, 2:3]; bb = cb[:, ch:ch + 1]
                        fw.op(DVE, lambda h, u=u, dst=dst, w2=w2, bb=bb: h.tensor_scalar(out=dst[:, 0:TP], in0=u[:, 2:2 + TP], scalar1=w2, scalar2=bb, op0=ALU.mult, op1=ALU.add),
                              reads=[ur, w_r], writes=[dr])
                        fw.op(DVE, lambda h, u=u, dst=dst, w1=w1: h.scalar_tensor_tensor(out=dst[:, 0:TP], in0=u[:, 1:1 + TP], scalar=w1, in1=dst[:, 0:TP], op0=ALU.mult, op1=ALU.add),
                              reads=[ur, w_r, dr], writes=[dr])
                        fw.op(DVE, lambda h, u=u, dst=dst, w0=w0: h.scalar_tensor_tensor(out=dst[:, 0:TP], in0=u[:, 0:TP], scalar=w0, in1=dst[:, 0:TP], op0=ALU.mult, op1=ALU.add),
                              reads=[ur, w_r, dr], writes=[dr])
                        fw.op(DVE, lambda h, dst=dst: h.memset(dst[:, TP:T], 0.0), writes=[dr])
                        for b in range(2):
                            d2 = dst[:, TP + 32 * b:TP + 32 * b + 16]
                            fw.op(DVE, lambda h, us=us, d2=d2, b=b, w2=w2, bb=bb: h.tensor_scalar(out=d2, in0=us[:, b, 2:18], scalar1=w2, scalar2=bb, op0=ALU.mult, op1=ALU.add),
                                  reads=[usr, w_r], writes=[dr])
                            fw.op(DVE, lambda h, us=us, d2=d2, b=b, w1=w1: h.scalar_tensor_tensor(out=d2, in0=us[:, b, 1:17], scalar=w1, in1=d2, op0=ALU.mult, op1=ALU.add),
                                  reads=[usr, w_r, dr], writes=[dr])
                            fw.op(DVE, lambda h, us=us, d2=d2, b=b, w0=w0: h.scalar_tensor_tensor(out=d2, in0=us[:, b, 0:16], scalar=w0, in1=d2, op0=ALU.mult, op1=ALU.add),
                                  reads=[usr, w_r, dr], writes=[dr])
                    fw.op(ACT, lambda h: h.activation(out=cg[0][:], in_=cg[0][:], func=AF.Gelu), reads=[cg_r[0]], writes=[cg_r[0]])
                    fw.op(DVE, lambda h, c=c: h.tensor_tensor(out=hT[:, c, :], in0=cg[0][:], in1=cg[1][:], op=ALU.mult), reads=[cg_r[0], cg_r[1]], writes=[h_r])
            with nc.allow_non_contiguous_dma(reason="conv-state outputs are token-major in DRAM"):
                for t in range(2):
                    fw.dma(SP, o_conv_p[l, t].rearrange("(c p) -> p c", p=128), ufin[:, :, t], reads=[uf_r], is_out=True)
                    for b in range(2):
                        fw.dma(SP, o_conv_s[l, b, t].rearrange("(c p) -> p c", p=128), ufs[:, :, b, t], reads=[uf_r], is_out=True)
            dense_to_z(l, 2, I.w_ffn_out[l], 44, lambda kg, t0, n: hT[:, kg, t0:t0 + n], [h_r])
            fw.barrier()
        layernorm(l, 2, l == DEPTH - 1)
    kaT_r = Res(); va_r = Res()
    SW = {}
    stg = [A(f"stg{i}", [128, 512], F32) for i in range(2)]; stg_r = [Res(), Res()]; stgi = [0]

    def STG():
        i = stgi[0]; stgi[0] = 1 - i
        return stg[i], stg_r[i]

    def xsrc(k, t0, n):
        return xT[:, k, t0:t0 + n]

    XT_ALL = xT_r + [xh_r]

    def swa_kv(l):
        sw_kaT = SW['kaT']; sw_va = SW['va']
        SUB2 = int(os.environ.get("MK_SUB2", "99"))
        W, Wr = wload(I.w_in[l, :, C_KA:C_KA + 256], KC, 256)
        if SUB2 == 0:
            return
        def cons_h(p, pr, t0, n):
            evac(DVE, sw_kaT[:, 0:128], p[:, 0:128], [pr], [kaT_r])
        proj_ws(W, Wr, KC, 0, 128, lambda k, t0, n: XH['t'][:, k, 384:512], [(0, 128)], cons_h, [xh_r])
        def cons_l(p, pr, t0, n):
            evac(ACT, sw_kaT[:, 128 + t0:128 + t0 + n], p[:, 0:n], [pr], [kaT_r])
        if SUB2 == 1:
            return
        proj_ws(W, Wr, KC, 0, 128, xsrc, [(0, 512), (512, 512), (TP, TS)], cons_l, xT_r)
        if SUB2 == 2:
            return
        for slot in range(10 if SUB2 > 4 else (9 if SUB2 == 4 else 8)):
            if slot == 0:
                lhs = lambda k: XH['t'][:, k, 384:512]; nt = 128; lr = [xh_r]
            else:
                tt = slot - 1; nt = NT[tt]
                lhs = lambda k, tt=tt, nt=nt: xT[:, k, TOK0[tt]:TOK0[tt] + nt]; lr = [xT_r[tt]]
            def cons(p, pr, slot=slot, nt=nt):
                if slot == 0:
                    fw.op(DVE, lambda h: h.tensor_scalar(out=sw_va[0:nt, slot, :], in0=p[0:nt, 128:256], scalar1=flag[0:nt, 0:1], scalar2=None, op0=ALU.mult),
                          reads=[pr, cres], writes=[va_r])
                else:
                    evac(DVE, sw_va[0:nt, slot, :], p[0:nt, 128:256], [pr], [va_r])
                if slot in (8, 9):
                    s, sr = STG()
                    evac(DVE, s[0:nt, 0:256], p[0:nt, 0:256], [pr], [sr])
                    for kv in range(2):
                        if slot == 8:
                            fw.dma(SP, o_swa_p[l, kv], s[0:128, kv * 128:(kv + 1) * 128], reads=[sr], is_out=True)
                        else:
                            fw.dma(SP, o_swa_s[l, kv], s[0:TS, kv * 128:(kv + 1) * 128], reads=[sr], is_out=True)
            proj_as(W, Wr, KC, 0, 256, lhs, nt, cons, lr)


    def swa_branch(l):
        load_sink(l)
        with ExitStack() as es:
            ebt5 = S(es, "ebt5", [128, 16, 256], BF16); ebt5s = S(es, "ebt5s", [128, 16, 32], BF16); ebn5 = S(es, "ebn5", [64, 16, 16], BF16)
            sw_kaT = S(es, "sw_kaT", [128, 128 + T], BF16); sw_va = S(es, "sw_va", [128, 10, 128], BF16)
            SW['kaT'] = sw_kaT; SW['va'] = sw_va
            build_t5(ebt5, ebt5s)
            with nc.allow_non_contiguous_dma(reason="tiny"):
                fw.dma(SP, ebn5[0:16, :, :], ebt5s[0:16, :, 16:32], reads=[eb_r], writes=[eb_r])
                fw.dma(SP, ebn5[32:48, :, :], ebt5s[0:16, :, 16:32], reads=[eb_r], writes=[eb_r])
            SUB = int(os.environ.get("MK_SUB", "99"))
            if SUB == 0:
                fw.barrier(); return
            swa_kv(l)
            if SUB == 1:
                fw.barrier(); return
            qaT = S(es, "qaT", [128, 8, T], BF16); qr = Res()
            ck = S(es, "ck", [128, 2, 128], BF16); ckT = S(es, "ckT", [128, 2, 128], BF16); cv = S(es, "cv", [128, 2, 128], BF16); cr_ = Res(); ctr = Res()
            for g in range(2):
                W, Wr = wload(I.w_in[l, :, C_QA + g * 512:C_QA + (g + 1) * 512], KC, 512)
                for s8 in range(8):
                    def cons(p, pr, t0, n, s8=s8, g=g):
                        evac(ACT if s8 % 2 else DVE, qaT[g * 64:(g + 1) * 64, s8, t0:t0 + n], p[g * 64:(g + 1) * 64, 0:n], [pr], [qr])
                    proj_ws(W, Wr, KC, s8 * 64, 64, xsrc, TOKBLK, cons, xT_r, pslice=(g * 64,))
            if SUB == 2:
                fw.barrier(); return
            fw.dma(POOL, ck[:], I.c_swa_k[l].rearrange("b k f -> k b f"), writes=[cr_])
            fw.dma(POOL, cv[:], I.c_swa_v[l].rearrange("b k f -> k b f"), writes=[cr_])
            for b in range(2):
                transpose_into(lambda j, b=b: ckT[:, b, :], ck[:, b, :], cr_, 128, 128, ctr)
            if SUB == 3:
                fw.barrier(); return
            for h in range(16 if SUB > 4 else 1):
                g = h // 8; gs = slice(g * 64, (g + 1) * 64); ob = (h % 2) * 64
                for qt in range(8):
                    kb_ = [(sw_kaT[gs, qt * 128:(qt + 1) * 128], 128, sw_va[:, qt, gs], ones_h[:] if qt == 0 else ones_b[:], None, 0),
                           (sw_kaT[gs, (qt + 1) * 128:(qt + 2) * 128], 128, sw_va[:, qt + 1, gs], ones_b[:], None, 0)]
                    attn(qaT[gs, h % 8, qt * 128:(qt + 1) * 128], kb_, 0.125, oT[ob:ob + 64, h // 2, qt * 128:(qt + 1) * 128], ob, 128,
                         [qr, kaT_r, va_r], oT_r, eb_all=ebt5[:, h, :], sink_ap=sinkx[ob:ob + 64, h:h + 1])
                for b in range(2):
                    c0 = TP + 32 * b; rs = slice(32 * b, 32 * b + 16)
                    kb_ = [(ckT[gs, b, :], 128, cv[:, b, gs], ones_b[:], ebt5s[:, h, 0:16], 0),
                           (sw_kaT[gs, 128 + c0:128 + c0 + 16], 16, sw_va[rs, 9, gs], ones_b[rs, :], ebn5[rs, h, :], 32 * b)]
                    attn(qaT[gs, h % 8, c0:c0 + 16], kb_, 0.125, oT[ob:ob + 64, h // 2, c0:c0 + 16], ob, 16,
                         [qr, kaT_r, va_r, ctr, cr_], oT_r, sink_ap=sinkx[ob:ob + 64, h:h + 1])
            fw.barrier()
        merge_branch(l, 0, C_GA, I.w_br_a)

    dbg = dout("dbg", [128, KC, T]) if os.environ.get("MK_DBG") else None

    def dump(t, r, nch):
        with ExitStack() as es:
            f = S(es, "dbgf", [128, T], F32); fr = Res()
            for k in range(nch):
                fw.op(DVE, lambda h, k=k: h.tensor_copy(out=f[:], in_=t[:, k, :]), reads=[r], writes=[fr])
                fw.dma(SP, dbg[:, k, :], f[:], reads=[fr], writes=[], is_out=True)
            fw.barrier()

    fw.barrier()

    for l in range(DEPTH):
        esm = ExitStack()
        mixT = S(esm, "mixT", [128, KC, T], BF16); oT = S(esm, "oT", [128, 8, T], BF16)
        fw.op(DVE, lambda h: h.memset(oT[:], 0.0), writes=[oT_r])
        fw.op(DVE, lambda h: h.memset(mixT[:], 0.0), writes=[mix_r])
        esx = ExitStack()
        XH['t'] = S(esx, "xhT", [128, KC, HALO], BF16)
        if l == 0 and STAGE >= 0:
            x_to_xT(l)
        if l > 0:
            if not SIM:
                fw.special(POOL, lambda h: h.indirect_dma_start(out=XH['t'][:].rearrange("p k t -> p (k t)"), out_offset=None, in_=ex_x_g.ap(),
                                                                in_offset=bass.IndirectOffsetOnAxis(ap=pidx[:, 0:1], axis=0)), reads=[ex_r, cres], writes=[xh_r])
            else:
                fw.op(DVE, lambda h: h.memset(XH['t'][:], 0.0), writes=[xh_r])
        if STAGE >= 2:
            swa_branch(l)
        if STAGE == 2:
            if dbg is not None:
                dump(mixT, mix_r, KC)
            break
        if STAGE >= 3:
            band_branch(l)
        fw.barrier(); esx.close()
        if STAGE >= 4:
            ret_branch(l)
        if STAGE == 4:
            if dbg is not None:
                dump(mixT, mix_r, KC)
            break
        if STAGE >= 5:
            dense_to_z(l, 0, I.w_mix_o[l], KC, lambda kg, t0, n: mixT[:, kg, t0:t0 + n], [mix_r])
            layernorm(l, 0, False)
        if STAGE == 5:
            if dbg is not None:
                dump(xT, xT_r[0], KC)
            break
        if STAGE >= 6:
            cross_attn(l, mixT)
            dense_to_z(l, 1, I.w_mo[l], KC, lambda kg, t0, n: mixT[:, kg, t0:t0 + n], [mix_r])
            fw.barrier(); esm.close()
            layernorm(l, 1, False)
        if STAGE == 6:
            if dbg is not None:
                dump(xT, xT_r[0], KC)
            break
        if STAGE >= 7:
            ffn_layer(l)
            if l < DEPTH - 1 and not SIM:
                with ExitStack() as esq:
                    xl = S(esq, "xl", [128, KC, HALO], BF16); xl_r = Res()
                    fw.op(DVE, lambda h: h.tensor_copy(out=xl[:], in_=xT[:, :, TP - HALO:TP]), reads=xT_r, writes=[xl_r])
                    br = Res()
                    fw.dma(POOL, ex_x_b.ap(), xl[:].rearrange("p k t -> p (k t)"), reads=[xl_r], writes=[br])
                    fw.special(POOL, lambda h: h.collective_compute("AllGather", ALU.bypass, replica_groups=[list(range(8))], ins=[ex_x_b.ap()], outs=[ex_x_g.ap()]),
                               reads=[br], writes=[ex_r], incval=1)
                    fw.barrier()
        if STAGE == 7 and l == 0:
            break
        if STAGE == 3:
            if dbg is not None:
                dump(mixT, mix_r, KC)
            break
        if STAGE < 2:
            break

    fw.finish()
    return nc


_NC_CACHE = {}


def _pad_s(a):
    out = np.zeros((TS,) + a.shape[2:], a.dtype)
    out[0:16] = a[0]; out[32:48] = a[1]
    return out


def kernel(**inputs):
    if "nc" not in _NC_CACHE:
        _NC_CACHE["nc"] = build_program()
    nc = _NC_CACHE["nc"]
    f = lambda a: np.ascontiguousarray(a, dtype=np.float32)
    I = {k: np.asarray(v) for k, v in inputs.items()}
    in_maps = []
    wnames = ["w_in", "t5_table", "swa_sink", "band_rel_table", "w_br_a", "w_br_b", "w_br_c", "w_mix_o", "ln1_g", "ln1_b", "ln2_g", "ln2_b",
              "ln3_g", "ln3_b", "w_mq", "w_mk", "w_mv", "w_mo", "w_ffn_in", "ffn_conv_w", "ffn_conv_b", "w_ffn_out"]
    shared = {k: f(I[k]) for k in wnames}
    for c in range(8):
        b = c // 4; seg = c % 4
        m = dict(shared)
        x = np.zeros((HALO + TP, D), np.float32)
        lo = seg * TP - HALO
        if lo >= 0:
            x[:] = I["x_prompt"][b, lo:lo + HALO + TP]
        else:
            x[HALO:] = I["x_prompt"][b, 0:TP]
        m["xp"] = x
        sb = slice(2 * c, 2 * c + 2)
        m["xs"] = _pad_s(f(I["x_sample"][sb]))
        m["mem"] = f(I["mem_prompt"][b])
        m["c_swa_k"] = f(I["cache_swa_k"][:, sb].reshape(2, 2, 128, 128)); m["c_swa_v"] = f(I["cache_swa_v"][:, sb].reshape(2, 2, 128, 128))
        m["c_ret"] = f(I["state_ret"][:, sb])
        m["c_band_k"] = f(I["cache_band_k"][:, sb].reshape(2, 2, 512, 1024)); m["c_band_v"] = f(I["cache_band_v"][:, sb].reshape(2, 2, 512, 1024))
        m["c_conv"] = f(I["state_ffn_conv"][:, sb])
        m["c_mem_k"] = f(I["cache_mem_k"][:, sb].reshape(2, 2, 256, 2048)); m["c_mem_v"] = f(I["cache_mem_v"][:, sb].reshape(2, 2, 256, 2048))
        m.update(core_consts(c))
        in_maps.append(m)
    used = set(nc._mk_inputs.keys())
    in_maps = [{k: v for k, v in m.items() if k in used} for m in in_maps]
    ncores = int(os.environ.get('MK_NCORES', '8'))
    res = run_bass_kernel_spmd(nc, in_maps[:ncores], core_ids=list(range(ncores)))
    if ncores < 8:
        return res.results
    R = res.results
    cat = np.concatenate
    y_p = np.stack([cat([R[4 * b + s]["o_yp"] for s in range(4)], 0) for b in range(2)], 0)
    uns = lambda a: np.stack([a[0:16], a[32:48]], 0)
    y_s = cat([uns(R[c]["o_ys"]) for c in range(8)], 0)
    last = [R[3], R[7]]; first = [R[0], R[4]]
    swa_k_p = np.stack([r["o_swa_p"][:, 0] for r in last], 1).reshape(2, 2, 128, 2, 64)
    swa_v_p = np.stack([r["o_swa_p"][:, 1] for r in last], 1).reshape(2, 2, 128, 2, 64)
    ret_p = np.stack([r["o_ret_p"] for r in last], 1)
    band_k_p = np.stack([r["o_band_p"][:, 0] for r in last], 1).reshape(2, 2, 512, 16, 64)
    band_v_p = np.stack([r["o_band_p"][:, 1] for r in last], 1).reshape(2, 2, 512, 16, 64)
    conv_p = np.stack([r["o_conv_p"] for r in last], 1)
    mem_k_p = np.stack([r["o_mem_p"][:, 0] for r in first], 1).reshape(2, 2, 256, 4, 512)
    mem_v_p = np.stack([r["o_mem_p"][:, 1] for r in first], 1).reshape(2, 2, 256, 4, 512)
    def samp(name, kv, tail):
        a = np.stack([np.stack([uns(R[c][name][d, kv]) for c in range(8)], 0).reshape((16, 16) + (-1,)) for d in range(2)], 0)
        return a.reshape((2, 16, 16) + tail)
    swa_k_s = samp("o_swa_s", 0, (2, 64)); swa_v_s = samp("o_swa_s", 1, (2, 64))
    band_k_s = samp("o_band_s", 0, (16, 64)); band_v_s = samp("o_band_s", 1, (16, 64))
    ret_s = cat([R[c]["o_ret_s"] for c in range(8)], 1)
    conv_s = cat([R[c]["o_conv_s"] for c in range(8)], 1)
    outs = (y_p, y_s, swa_k_p, swa_v_p, ret_p, band_k_p, band_v_p, conv_p, mem_k_p, mem_v_p,
            swa_k_s, swa_v_s, ret_s, band_k_s, band_v_s, conv_s)
    return tuple(np.ascontiguousarray(o, dtype=np.float32) for o in outs)
```
